# Optimizing a Trainium2 kernel written in Bass

```python
import math
import jax, jax.numpy as jnp
from jax import lax
import numpy as np

D_MODEL = 2048
BATCH = 4
SEQ = 4096
DEPTH = 4

N_A = DEPTH // 2
N_B = DEPTH - N_A
N_META = 16
CONV_WIDTH = 3
HEAD_DIM = 128
N_HEADS = D_MODEL // HEAD_DIM
D_FF = ((8 * D_MODEL // 3 + 255) // 256) * 256
BLOCK = 128
PAD = BLOCK - N_META
EPS = 1e-6
NEG = -1e30

kernel_name = "yoco_shortconv_forgetting_attn_meta"


def rms_norm(x, g):
    xf = x.astype(jnp.float32)
    y = xf * lax.rsqrt(jnp.mean(xf * xf, axis=-1, keepdims=True) + EPS)
    return (y * g.astype(jnp.float32)).astype(x.dtype)


def short_conv_mixer(xn, w_in, conv_w, w_out):
    L = xn.shape[1]
    b_gate, c_gate, h = jnp.split(xn @ w_in, 3, axis=-1)
    u = jnp.pad(c_gate * h, ((0, 0), (CONV_WIDTH - 1, 0), (0, 0)))
    conv = sum(u[:, j:j + L, :] * conv_w[j] for j in range(CONV_WIDTH))
    return (b_gate * conv) @ w_out


def swiglu(xn, w_gu, w_down):
    g, u = jnp.split(xn @ w_gu, 2, axis=-1)
    return (jax.nn.silu(g) * u) @ w_down


def shared_kv(h, kv_norm, w_kv, k_norm, w_f, b_f):
    Bsz, L, _ = h.shape
    xn = rms_norm(h, kv_norm)
    k, v = jnp.split(xn @ w_kv, 2, axis=-1)
    k = rms_norm(k.reshape(Bsz, L, N_HEADS, HEAD_DIM), k_norm)
    v = v.reshape(Bsz, L, N_HEADS, HEAD_DIM)
    log_f = jax.nn.log_sigmoid((xn @ w_f + b_f).astype(jnp.float32))
    pad4 = ((0, 0), (PAD, 0), (0, 0), (0, 0))
    k = jnp.pad(k, pad4)
    v = jnp.pad(v, pad4)
    c = jnp.cumsum(jnp.pad(log_f, ((0, 0), (PAD, 0), (0, 0))), axis=1)
    return k, v, jnp.transpose(c, (0, 2, 1))


def forgetting_attention(xn, w_q, q_norm, w_o, k, v, c):
    Bsz, L, _ = xn.shape
    Lp = k.shape[1]
    q = rms_norm((xn @ w_q).reshape(Bsz, L, N_HEADS, HEAD_DIM), q_norm)
    q = jnp.pad(q, ((0, 0), (PAD, 0), (0, 0), (0, 0)))
    scale = 1.0 / math.sqrt(HEAD_DIM)
    kpos = jnp.arange(Lp)

    def block(i):
        start = i * BLOCK
        qb = lax.dynamic_slice_in_dim(q, start, BLOCK, axis=1)
        cq = lax.dynamic_slice_in_dim(c, start, BLOCK, axis=2)
        s = jnp.einsum('bqhd,bkhd->bhqk', qb, k).astype(jnp.float32) * scale
        s = s + (cq[..., :, None] - c[..., None, :])
        qpos = start + jnp.arange(BLOCK)
        mask = (kpos[None, :] <= qpos[:, None]) & (kpos[None, :] >= PAD)
        s = jnp.where(mask, s, NEG)
        p = jax.nn.softmax(s, axis=-1).astype(v.dtype)
        return jnp.einsum('bhqk,bkhd->bqhd', p, v)

    o = lax.map(block, jnp.arange(Lp // BLOCK))
    o = jnp.transpose(o, (1, 0, 2, 3, 4)).reshape(Bsz, Lp, N_HEADS * HEAD_DIM)[:, PAD:]
    return o @ w_o


def setup_inputs(seed: int = 0) -> dict:
    key = jax.random.key(seed)
    ks = jax.random.split(key, 24)
    D, F, H, Dh = D_MODEL, D_FF, N_HEADS, HEAD_DIM
    nrm = lambda k, shape, s: jax.random.normal(k, shape, jnp.float32) * s
    gain = lambda k, shape: 1.0 + nrm(k, shape, 0.02)
    out_s = 0.5 / math.sqrt(DEPTH)
    return {
        "x": nrm(ks[0], (BATCH, SEQ, D), 1.0),
        "meta": nrm(ks[1], (N_META, D), 1.0),
        "a_norm": gain(ks[2], (N_A, D)),
        "a_w_in": nrm(ks[3], (N_A, D, 3 * D), D ** -0.5),
        "a_conv": nrm(ks[4], (N_A, CONV_WIDTH, D), CONV_WIDTH ** -0.5),
        "a_w_out": nrm(ks[5], (N_A, D, D), D ** -0.5 * out_s),
        "kv_norm": gain(ks[6], (D,)),
        "w_kv": nrm(ks[7], (D, 2 * H * Dh), D ** -0.5),
        "k_norm": gain(ks[8], (Dh,)),
        "w_f": nrm(ks[9], (D, H), D ** -0.5),
        "b_f": 3.0 + nrm(ks[10], (H,), 0.5),
        "b_norm": gain(ks[11], (N_B, D)),
        "b_w_q": nrm(ks[12], (N_B, D, H * Dh), D ** -0.5),
        "b_q_norm": gain(ks[13], (N_B, Dh)),
        "b_w_o": nrm(ks[14], (N_B, H * Dh, D), D ** -0.5 * out_s),
        "ffn_norm": gain(ks[15], (DEPTH, D)),
        "ffn_w_gu": nrm(ks[16], (DEPTH, D, 2 * F), D ** -0.5),
        "ffn_w_down": nrm(ks[17], (DEPTH, F, D), F ** -0.5 * out_s),
    }


def reference(x, meta, a_norm, a_w_in, a_conv, a_w_out, kv_norm, w_kv, k_norm, w_f, b_f,
              b_norm, b_w_q, b_q_norm, b_w_o, ffn_norm, ffn_w_gu, ffn_w_down):
    Bsz = x.shape[0]
    meta_b = jnp.broadcast_to(meta.astype(x.dtype)[None], (Bsz, N_META, D_MODEL))
    h = jnp.concatenate([meta_b, x], axis=1)
    k = v = c = None
    for layer in range(DEPTH):
        if layer < N_A:
            h = h + short_conv_mixer(rms_norm(h, a_norm[layer]), a_w_in[layer],
                                     a_conv[layer], a_w_out[layer])
        else:
            if layer == N_A:
                k, v, c = shared_kv(h, kv_norm, w_kv, k_norm, w_f, b_f)
            j = layer - N_A
            h = h + forgetting_attention(rms_norm(h, b_norm[j]), b_w_q[j], b_q_norm[j],
                                         b_w_o[j], k, v, c)
        h = h + swiglu(rms_norm(h, ffn_norm[layer]), ffn_w_gu[layer], ffn_w_down[layer])
    return h[:, N_META:, :]
```

```python
import contextlib
import numpy as np
import ml_dtypes
import concourse.bass as bass
import concourse.mybir as mybir
from concourse.bass_utils import run_bass_kernel_spmd

F32 = mybir.dt.float32
BF16 = mybir.dt.bfloat16
AF = mybir.ActivationFunctionType
ALU = mybir.AluOpType

ENGS = ("sync", "scalar", "vector", "gpsimd", "tensor")
SEM_ROLL = 24000
EPS = 1e-6
MASKV = 60000.0


class _Op:
    __slots__ = ("eng", "fn", "deps", "is_dma", "key", "sem", "semval", "signal", "inc")

    def __init__(self, eng, fn, is_dma, key, inc=16):
        self.inc = inc
        self.eng = eng
        self.fn = fn
        self.deps = []
        self.is_dma = is_dma
        self.key = key
        self.sem = None
        self.semval = 0
        self.signal = False


class Sched:
    def __init__(self, nc, stack):
        self.nc = nc
        self.stack = stack
        self.q = {e: [] for e in ENGS}
        self.lastw = {}
        self.readers = {}
        self.nsem = 0
        self.nops = 0

    def _newsem(self, name):
        self.nsem += 1
        return self.stack.enter_context(self.nc.semaphore(f"s{self.nsem}_{name}"))

    def add(self, eng, fn, reads=(), writes=(), dma_key=None, inc=16):
        op = _Op(eng, fn, dma_key is not None, dma_key, inc)
        self.nops += 1
        deps = set()
        for k in reads:
            w = self.lastw.get(k)
            if w is not None:
                deps.add(w)
        for k in writes:
            w = self.lastw.get(k)
            if w is not None:
                deps.add(w)
            for r in self.readers.get(k, ()):
                deps.add(r)
        for d in deps:
            if d is op:
                continue
            if d.eng == "tensor" and eng == "tensor" and not d.is_dma:
                continue
            op.deps.append(d)
            d.signal = True
        for k in writes:
            self.lastw[k] = op
            self.readers[k] = []
        for k in reads:
            self.readers.setdefault(k, []).append(op)
        self.q[eng].append(op)
        return op

    def emit(self):
        nc = self.nc
        dma_sems = {}
        dma_cnt = {}
        for e in ENGS:
            sem = None
            cnt = 0
            for op in self.q[e]:
                if op.is_dma:
                    if op.key not in dma_sems:
                        dma_sems[op.key] = self._newsem("d")
                        dma_cnt[op.key] = 0
                    dma_cnt[op.key] += op.inc
                    op.sem = dma_sems[op.key]
                    op.semval = dma_cnt[op.key]
                elif op.signal:
                    if sem is None or cnt >= SEM_ROLL:
                        sem = self._newsem(e)
                        cnt = 0
                    cnt += 1
                    op.sem = sem
                    op.semval = cnt
        q = self.q

        def run(e, eng):
            waited = {}
            for op in q[e]:
                need = {}
                for d in op.deps:
                    sid = id(d.sem)
                    if waited.get(sid, 0) >= d.semval:
                        continue
                    if sid not in need or need[sid][1] < d.semval:
                        need[sid] = (d.sem, d.semval)
                for sid, (s, v) in need.items():
                    eng.wait_ge(s, v)
                    waited[sid] = v
                if op.fn is None:
                    continue
                ins = op.fn(eng)
                if op.is_dma:
                    if op.inc == 16:
                        ins.then_inc(op.sem, 16)
                    else:
                        ins.then_inc(op.sem)
                elif op.signal:
                    ins.then_inc(op.sem, 1)

        with nc.Block() as block:
            @block.sync
            def _(eng):
                run("sync", eng)

            @block.scalar
            def _(eng):
                run("scalar", eng)

            @block.vector
            def _(eng):
                run("vector", eng)

            @block.gpsimd
            def _(eng):
                run("gpsimd", eng)

            @block.tensor
            def _(eng):
                run("tensor", eng)


class Cfg:
    def __init__(self, D=2048, F=5632, NMAIN=2048, TT=512, NL_A=2, NL_B=2):
        self.D = D
        self.F = F
        self.KC = D // 128
        self.FC = F // 128
        self.H = D // 128
        self.NMAIN = NMAIN
        self.TT = TT
        self.NTILE = NMAIN // TT
        self.NCOL = 128 + 2 * NMAIN
        self.NBLK = self.NCOL // 128
        self.OWN0 = 128 + NMAIN
        self.PB = self.OWN0 // 128
        self.NL_A = NL_A
        self.NL_B = NL_B
        self.SW = 2 * TT
        self.XW = 128 + self.SW
        self.XWB = self.SW
        self.ST_A = [(0, 128 + self.SW)] + [(128 + self.SW * i, self.SW) for i in range(1, 2 * NMAIN // self.SW)]
        self.ST_B = [(self.OWN0 + self.SW * i, self.SW) for i in range(NMAIN // self.SW)]
        self.ST = self.ST_A
        self.FSPLIT = 2 if self.FC % 2 == 0 else 1
        self.FH = self.FC // self.FSPLIT
        self.VGC = min(256, D)
        self.VG = D // self.VGC

    def set_mode(self, mode):
        self.ST = self.ST_A if mode == "A" else self.ST_B

    def tiles(self, s, with_prefix=True, kvmode=False):
        base, n = self.ST[s]
        out = []
        c = base
        if s == 0 and base == 0:
            if kvmode:
                out.append((0, 128))
            elif with_prefix:
                out.append((112, 16))
            c = 128
        while c < base + n:
            out.append((c, self.TT))
            c += self.TT
        return out


def blk(W, gc=128):
    Din, Dout = W.shape
    return np.ascontiguousarray(
        W.reshape(Din // 128, 128, Dout // gc, gc).transpose(2, 1, 0, 3).reshape(Dout // gc, 128, (Din // 128) * gc))


def vecT(g):
    g = np.asarray(g, np.float32)
    kc = g.shape[-1] // 128
    a = g.reshape(-1, kc, 128)
    return np.ascontiguousarray(a.transpose(2, 0, 1).reshape(128, -1))


class Prog:
    def __init__(self, cfg):
        self.cfg = cfg
        cfg.set_mode("A")
        self.mode = "R"
        self.nc = bass.Bass("TRN2", target_bir_lowering=False)
        self.stack = contextlib.ExitStack()
        self.S = Sched(self.nc, self.stack)
        self.wi = 0
        self.cnt = {}

    def din(self, name, shape, dt=F32):
        return self.nc.dram_tensor(name, list(shape), dt, kind="ExternalInput").ap()

    def dout(self, name, shape, dt=F32):
        return self.nc.dram_tensor(name, list(shape), dt, kind="ExternalOutput").ap()

    def sb(self, name, shape, dt):
        return self.stack.enter_context(self.nc.sbuf_tensor("sb_" + name, list(shape), dt))

    def rot(self, name, n):
        i = self.cnt.get(name, 0)
        self.cnt[name] = i + 1
        return i % n

    def dma(self, eng, out, in_, reads, writes, key):
        return self.S.add(eng, lambda e: e.dma_start(out=out, in_=in_), reads=reads, writes=writes, dma_key=key)

    def mm(self, out, lhsT, rhs, start, stop, reads, writes):
        return self.S.add("tensor", lambda e: e.matmul(out, lhsT=lhsT, rhs=rhs, start=start, stop=stop),
                          reads=reads, writes=writes)

    def act(self, out, in_, func, reads, writes, bias=None, scale=None):
        kw = {}
        if bias is not None:
            kw["bias"] = bias
        if scale is not None:
            kw["scale"] = scale
        return self.S.add("scalar", lambda e: e.activation(out=out, in_=in_, func=func, **kw), reads=reads, writes=writes)

    def build(self):
        with self.stack:
            self._build()
            self.S.emit()
        return self.nc

    def _build(self):
        c = self.cfg
        nc = self.nc
        KC, FC, H, D, F = c.KC, c.FC, c.H, c.D, c.F
        NL = c.NL_A + c.NL_B
        self.xT = self.din("xT", [D, c.NCOL])
        self.hT = nc.dram_tensor("hT_scr", [D, c.NCOL], F32).ap()
        self.outT = self.dout("outT", [D, c.NMAIN])
        self.KT_own = nc.dram_tensor("kt_scr", [H * 128, c.NCOL], BF16).ap()
        self.V_own = nc.dram_tensor("v_scr", [H * 128, c.NBLK * 128], BF16).ap()
        self.a_norm = self.din("a_norm_T", [128, c.NL_A * KC])
        self.a_w_in = self.din("a_w_in_b", [c.NL_A * 3 * KC, 128, KC * 128])
        self.a_conv = self.din("a_conv_T", [128, c.NL_A * 3 * KC])
        self.a_w_out = self.din("a_w_out_b", [c.NL_A * KC, 128, KC * 128])
        self.kv_norm = self.din("kv_norm_T", [128, KC])
        self.w_k = self.din("w_k_b", [H, 128, KC * 128])
        self.w_v = self.din("w_v_b", [c.VG, 128, KC * c.VGC])
        self.k_norm = self.din("k_norm_T", [128, 1])
        self.w_f = self.din("w_f_l", [128, KC * H])
        self.b_f = self.din("b_f_b", [128, H])
        self.tri32_d = self.din("tri32", [128, 128])
        self.b_norm = self.din("b_norm_T", [128, c.NL_B * KC])
        self.b_w_q = self.din("b_w_q_b", [c.NL_B * H, 128, KC * 128])
        self.b_q_norm = self.din("b_q_norm_T", [128, c.NL_B])
        self.b_w_o = self.din("b_w_o_b", [c.NL_B * KC, 128, KC * 128])
        self.koff = self.din("koff", [128, c.NBLK])
        self.sel32_d = self.din("sel32", [128, 128])
        self.trim_d = self.din("trimask", [128, 128])
        self.ffn_norm = self.din("ffn_norm_T", [128, NL * KC])
        self.ffn_w_gu = self.din("ffn_w_gu_b", [NL * 2 * FC, 128, KC * 128])
        self.ffn_w_down = self.din("ffn_w_down_b", [NL * KC, 128, FC * 128])

        TT = c.TT
        self.xn = self.sb("xn", [128, KC * c.XW], BF16)
        bigw = max(c.FH * c.XW, KC * c.XW, (H + 2) * c.XWB + 4 * c.NBLK * 128)
        self.big = self.sb("big", [128, bigw], BF16)
        self.WSLOT = 6
        self.WW = max(KC * c.VGC, c.FH * 128, KC * 128)
        self.wbuf = [self.sb(f"wbuf{i}", [128, self.WW], BF16) for i in range(self.WSLOT)]
        self.NXS = 4
        self.xs = [self.sb(f"xs{i}", [128, TT], F32) for i in range(self.NXS)]
        self.sqb = [self.sb(f"sqb{i}", [128, TT], BF16) for i in range(2)]
        self.rstd = self.sb("rstd", [128, TT], F32)
        self.tmpa = [self.sb(f"tmpa{i}", [128, TT], F32) for i in range(2)]
        self.tmpb = [self.sb(f"tmpb{i}", [128, TT], F32) for i in range(2)]
        self.ub = [self.sb(f"ub{i}", [128, TT + 2], F32) for i in range(2)]
        self.acc = [self.sb(f"acc{i}", [128, TT], F32) for i in range(2)]
        self.hs = [self.sb(f"hs{i}", [128, TT], F32) for i in range(3)]
        self.st16 = [self.sb(f"st16_{i}", [128, TT], BF16) for i in range(3)]
        self.uh = self.sb("uh", [128, 2 * KC * 2], F32)
        self.ones_bf = self.sb("ones_bf", [128, 128], BF16)
        self.gvec = self.sb("gvec", [128, 16 * KC + 64], F32)
        self.ps = [self.stack.enter_context(nc.psum_tensor(f"ps{i}", [128, 512], F32)) for i in range(8)]
        self.S.add("vector", lambda e: e.memset(self.ones_bf[:], 1.0), writes=["ones_bf"])
        self._build_A()
        c.set_mode("B")
        self._build_B()

    def load_vec(self, name, dram, ncols, off):
        dst = self.gvec[:, off:off + ncols]
        self.dma("sync", dst, dram, reads=[], writes=[name], key=name)
        return dst

    def load_w(self, dram_ap, width, tag):
        i = self.wi % self.WSLOT
        self.wi += 1
        key = f"wbuf{i}"
        dst = self.wbuf[i][:, 0:width]
        self.dma("gpsimd", dst, dram_ap, reads=[], writes=[key], key=key)
        return self.wbuf[i], key

    def hkey(self, m, c0):
        return f"h{m}_{c0}"

    def xkey(self, m, c0):
        return f"x{m}_{c0}"

    def norm(self, src, srckeyf, gname, goff, tiles, base, zero_first=None):
        c = self.cfg
        KC = c.KC
        S = self.S
        for (c0, n) in tiles:
            lc = c0 - base
            ss = self.ps[7]
            for k in range(KC):
                i = self.rot("xs", self.NXS)
                xs = self.xs[i]
                rk = srckeyf(k, c0)
                rk = rk if isinstance(rk, list) else [rk]
                self.dma("sync", xs[:, 0:n], src[k * 128:(k + 1) * 128, c0:c0 + n],
                         reads=rk, writes=[f"xs{i}"], key=f"xs{i}")
                j = self.rot("sqb", 2)
                sq = self.sqb[j]
                self.act(sq[:, 0:n], xs[:, 0:n], AF.Square, reads=[f"xs{i}"], writes=[f"sqb{j}"])
                self.mm(ss[:, 0:n], self.ones_bf[:], sq[:, 0:n], k == 0, k == KC - 1,
                        reads=[f"sqb{j}", "ones_bf"], writes=["ps7"])
                g = self.gvec[:, goff + k:goff + k + 1]
                dst = self.xn[:, k * c.XW + lc:k * c.XW + lc + n]
                S.add("vector", lambda e, dst=dst, xs=xs, g=g, n=n: e.tensor_scalar(
                    out=dst, in0=xs[:, 0:n], scalar1=g, scalar2=None, op0=ALU.mult),
                    reads=[f"xs{i}", gname], writes=[f"xn{k}_{c0}"])
            self.act(self.rstd[:, 0:n], ss[:, 0:n], AF.Sqrt, reads=["ps7"], writes=["rstd"],
                     bias=EPS, scale=1.0 / c.D)
            S.add("vector", lambda e, n=n: e.reciprocal(out=self.rstd[:, 0:n], in_=self.rstd[:, 0:n]),
                  reads=["rstd"], writes=["rstd"])
            for k in range(KC):
                dst = self.xn[:, k * c.XW + lc:k * c.XW + lc + n]
                eng = "vector" if k % 2 == 0 else "gpsimd"
                S.add(eng, lambda e, dst=dst, n=n: e.tensor_tensor(
                    out=dst, in0=dst, in1=self.rstd[:, 0:n], op=ALU.mult),
                    reads=[f"xn{k}_{c0}", "rstd"], writes=[f"xn{k}_{c0}"])

    def resid(self, pst, pkey, src, srckeyf, dst, dstkeyf, m, c0, n, dst_c0=None):
        S = self.S
        i = self.rot("hs", 3)
        hs = self.hs[i]
        self.dma("sync", hs[:, 0:n], src[m * 128:(m + 1) * 128, c0:c0 + n],
                 reads=[srckeyf(m, c0)], writes=[f"hs{i}"], key=f"hs{i}_ld")
        S.add("vector", lambda e: e.tensor_tensor(out=hs[:, 0:n], in0=hs[:, 0:n], in1=pst[:, 0:n], op=ALU.add),
              reads=[f"hs{i}", pkey], writes=[f"hs{i}"])
        dc = c0 if dst_c0 is None else dst_c0
        self.dma("sync", dst[m * 128:(m + 1) * 128, dc:dc + n], hs[:, 0:n],
                 reads=[f"hs{i}"], writes=[dstkeyf(m, c0)], key=f"hs{i}_st")

    def hkey(self, m, c0):
        return f"h{m}_{c0}"

    def xkey(self, m, c0):
        return f"x{m}_{c0}"

    def resid(self, pst, pkey, src, srckeyf, dst, dstkeyf, m, c0, n, dst_c0=None):
        S = self.S
        i = self.rot("hs", 3)
        hs = self.hs[i]
        self.dma("sync", hs[:, 0:n], src[m * 128:(m + 1) * 128, c0:c0 + n],
                 reads=[srckeyf(m, c0)], writes=[f"hs{i}"], key=f"hs{i}_ld")
        S.add("vector", lambda e: e.tensor_tensor(out=hs[:, 0:n], in0=hs[:, 0:n], in1=pst[:, 0:n], op=ALU.add),
              reads=[f"hs{i}", pkey], writes=[f"hs{i}"])
        dc = c0 if dst_c0 is None else dst_c0
        self.dma("sync", dst[m * 128:(m + 1) * 128, dc:dc + n], hs[:, 0:n],
                 reads=[f"hs{i}"], writes=[dstkeyf(m, c0)], key=f"hs{i}_st")

    def ffn(self, l, s, src, srckeyf, dst_final=None):
        c = self.cfg
        KC, FC, FH = c.KC, c.FC, c.FH
        S = self.S
        base, _ = c.ST[s]
        tiles = c.tiles(s)

        def loadf(f):
            return (self.load_w(self.ffn_w_gu[l * 2 * FC + f], KC * 128, "g"),
                    self.load_w(self.ffn_w_gu[l * 2 * FC + FC + f], KC * 128, "u"))
        pre = {0: loadf(0)}
        self.norm(src, srckeyf, "ffn_norm", self.off_ffn + l * KC, tiles, base)
        XW = c.XW
        cur_src, cur_key = src, srckeyf
        for fh in range(c.FSPLIT):
            for fi in range(FH):
                f = fh * FH + fi
                (wt, wkey), (wt2, wkey2) = pre.pop(f) if f in pre else loadf(f)
                for (c0, n) in tiles:
                    lc = c0 - base
                    r = self.rot("ffnps", 2)
                    pg, pu = self.ps[2 * r], self.ps[2 * r + 1]
                    for k in range(KC):
                        self.mm(pg[:, 0:n], wt[:, k * 128:(k + 1) * 128], self.xn[:, k * XW + lc:k * XW + lc + n],
                                k == 0, k == KC - 1, reads=[wkey, f"xn{k}_{c0}"], writes=[f"ps{2 * r}"])
                    for k in range(KC):
                        self.mm(pu[:, 0:n], wt2[:, k * 128:(k + 1) * 128], self.xn[:, k * XW + lc:k * XW + lc + n],
                                k == 0, k == KC - 1, reads=[wkey2, f"xn{k}_{c0}"], writes=[f"ps{2 * r + 1}"])
                    j = self.rot("tmpa", 2)
                    sg = self.tmpa[j]
                    self.act(sg[:, 0:n], pg[:, 0:n], AF.Silu, reads=[f"ps{2 * r}"], writes=[f"tmpa{j}"])
                    dst = self.big[:, fi * XW + lc:fi * XW + lc + n]
                    S.add("vector", lambda e, dst=dst, sg=sg, pu=pu, n=n: e.tensor_tensor(
                        out=dst, in0=sg[:, 0:n], in1=pu[:, 0:n], op=ALU.mult),
                        reads=[f"tmpa{j}", f"ps{2 * r + 1}"], writes=[f"big{fi}_{c0}"])
            last = fh == c.FSPLIT - 1
            for m in range(KC):
                wt, wkey = self.load_w(self.ffn_w_down[l * KC + m][:, fh * FH * 128:(fh + 1) * FH * 128], FH * 128, "d")
                for (c0, n) in tiles:
                    lc = c0 - base
                    r = self.rot("dps", 2)
                    po = self.ps[4 + r]
                    for fi in range(FH):
                        self.mm(po[:, 0:n], wt[:, fi * 128:(fi + 1) * 128], self.big[:, fi * XW + lc:fi * XW + lc + n],
                                fi == 0, fi == FH - 1, reads=[wkey, f"big{fi}_{c0}"], writes=[f"ps{4 + r}"])
                    if last and dst_final is not None:
                        self.resid(po, f"ps{4 + r}", cur_src, cur_key, dst_final, lambda m, c0: f"out{m}_{c0}",
                                   m, c0, n, dst_c0=c0 - c.OWN0)
                    else:
                        self.resid(po, f"ps{4 + r}", cur_src, cur_key, self.hT, self.hkey, m, c0, n)
            cur_src, cur_key = self.hT, self.hkey

    def mixer_a(self, l, s, src, srckeyf):
        c = self.cfg
        KC = c.KC
        S = self.S
        XW = c.XW
        base, _ = c.ST[s]
        tiles = c.tiles(s)

        def loadm(m):
            return [self.load_w(self.a_w_in[(l * 3 + part) * KC + m], KC * 128, "in") for part in range(3)]
        nxt = loadm(0)
        self.norm(src, srckeyf, "a_norm", self.off_an + l * KC, tiles, base)
        uo = l * KC * 2
        if s == 0:
            S.add("vector", lambda e, uo=uo: e.memset(self.uh[:, uo:uo + 2 * KC], 0.0), writes=[f"uh{l}_{m}" for m in range(KC)])
        for m in range(KC):
            wts = nxt
            if m + 1 < KC:
                nxt = loadm(m + 1)
            cw = [self.gvec[:, self.off_cv + (l * 3 + j) * KC + m:self.off_cv + (l * 3 + j) * KC + m + 1] for j in range(3)]
            for (c0, n) in tiles:
                lc = c0 - base
                r = self.rot("aps", 2)
                pp = [self.ps[3 * r + q] for q in range(3)]
                for part in range(3):
                    wt, wkey = wts[part]
                    for k in range(KC):
                        self.mm(pp[part][:, 0:n], wt[:, k * 128:(k + 1) * 128], self.xn[:, k * XW + lc:k * XW + lc + n],
                                k == 0, k == KC - 1, reads=[wkey, f"xn{k}_{c0}"], writes=[f"ps{3 * r + part}"])
                pb, pc, ph = pp
                j = self.rot("tmpa", 2)
                csb = self.tmpa[j]
                self.act(csb[:, 0:n], pc[:, 0:n], AF.Copy, reads=[f"ps{3 * r + 1}"], writes=[f"tmpa{j}"])
                ui = self.rot("ub", 2)
                ub = self.ub[ui]
                S.add("scalar", lambda e, ub=ub, m=m, uo=uo: e.activation(out=ub[:, 0:2], in_=self.uh[:, uo + 2 * m:uo + 2 * m + 2], func=AF.Copy),
                      reads=[f"uh{l}_{m}"], writes=[f"ub{ui}"])
                S.add("vector", lambda e, ub=ub, csb=csb, ph=ph, n=n: e.tensor_tensor(
                    out=ub[:, 2:2 + n], in0=csb[:, 0:n], in1=ph[:, 0:n], op=ALU.mult),
                    reads=[f"tmpa{j}", f"ps{3 * r + 2}"], writes=[f"ub{ui}"])
                S.add("scalar", lambda e, ub=ub, m=m, n=n, uo=uo: e.activation(out=self.uh[:, uo + 2 * m:uo + 2 * m + 2], in_=ub[:, n:n + 2], func=AF.Copy),
                      reads=[f"ub{ui}"], writes=[f"uh{l}_{m}"])
                ai = self.rot("acc", 2)
                acc = self.acc[ai]
                S.add("scalar", lambda e, acc=acc, ub=ub, n=n, w=cw[2]: e.activation(
                    out=acc[:, 0:n], in_=ub[:, 2:2 + n], func=AF.Copy, scale=w),
                    reads=[f"ub{ui}", "a_conv"], writes=[f"acc{ai}"])
                S.add("vector", lambda e, acc=acc, ub=ub, n=n, w=cw[1]: e.scalar_tensor_tensor(
                    out=acc[:, 0:n], in0=ub[:, 1:1 + n], scalar=w, in1=acc[:, 0:n], op0=ALU.mult, op1=ALU.add),
                    reads=[f"ub{ui}", f"acc{ai}", "a_conv"], writes=[f"acc{ai}"])
                S.add("vector", lambda e, acc=acc, ub=ub, n=n, w=cw[0]: e.scalar_tensor_tensor(
                    out=acc[:, 0:n], in0=ub[:, 0:n], scalar=w, in1=acc[:, 0:n], op0=ALU.mult, op1=ALU.add),
                    reads=[f"ub{ui}", f"acc{ai}", "a_conv"], writes=[f"acc{ai}"])
                dst = self.big[:, m * XW + lc:m * XW + lc + n]
                S.add("vector", lambda e, dst=dst, acc=acc, pb=pb, n=n: e.tensor_tensor(
                    out=dst, in0=acc[:, 0:n], in1=pb[:, 0:n], op=ALU.mult),
                    reads=[f"acc{ai}", f"ps{3 * r}"], writes=[f"big{m}_{c0}"])
        for m in range(KC):
            wt, wkey = self.load_w(self.a_w_out[l * KC + m], KC * 128, "o")
            for (c0, n) in tiles:
                lc = c0 - base
                r = self.rot("ops", 2)
                po = self.ps[6 + r]
                for k in range(KC):
                    self.mm(po[:, 0:n], wt[:, k * 128:(k + 1) * 128], self.big[:, k * XW + lc:k * XW + lc + n],
                            k == 0, k == KC - 1, reads=[wkey, f"big{k}_{c0}"], writes=[f"ps{6 + r}"])
                self.resid(po, f"ps{6 + r}", src, srckeyf, self.hT, self.hkey, m, c0, n)

    def qk_norm_store(self, pk, pkkey, n, gap, gname, dst, dstkey):
        S = self.S
        j = self.rot("tmpa", 2)
        raw = self.tmpa[j]
        self.act(raw[:, 0:n], pk[:, 0:n], AF.Copy, reads=[pkkey], writes=[f"tmpa{j}"])
        q = self.rot("sqb", 2)
        sq = self.sqb[q]
        self.act(sq[:, 0:n], pk[:, 0:n], AF.Square, reads=[pkkey], writes=[f"sqb{q}"])
        r = self.rot("ssps", 2)
        pss = self.ps[2 + r]
        self.mm(pss[:, 0:n], self.ones_bf[:], sq[:, 0:n], True, True, reads=[f"sqb{q}", "ones_bf"], writes=[f"ps{2 + r}"])
        b = self.rot("tmpb", 2)
        t = self.tmpb[b]
        self.act(t[:, 0:n], pss[:, 0:n], AF.Sqrt, reads=[f"ps{2 + r}"], writes=[f"tmpb{b}"], bias=EPS, scale=1.0 / 128)
        S.add("vector", lambda e: e.reciprocal(out=t[:, 0:n], in_=t[:, 0:n]), reads=[f"tmpb{b}"], writes=[f"tmpb{b}"])
        S.add("vector", lambda e: e.scalar_tensor_tensor(out=dst, in0=raw[:, 0:n], scalar=gap, in1=t[:, 0:n],
                                                         op0=ALU.mult, op1=ALU.mult),
              reads=[f"tmpa{j}", f"tmpb{b}", gname], writes=[dstkey])

    def kv_phase(self, s):
        c = self.cfg
        KC, H = c.KC, c.H
        S = self.S
        XW = c.XW
        base, _ = c.ST[s]
        tiles = c.tiles(s, kvmode=True)
        self.norm(self.hT, self.hkey_kv, "kv_norm", self.off_kvn, tiles, base)
        xk = self.xnkey_kv
        pending = None
        for h in range(H):
            wt, wkey = self.load_w(self.w_k[h], KC * 128, "k")
            for (c0, n) in tiles:
                lc = c0 - base
                r = self.rot("kps", 2)
                pk = self.ps[r]
                for k in range(KC):
                    self.mm(pk[:, 0:n], wt[:, k * 128:(k + 1) * 128], self.xn[:, k * XW + lc:k * XW + lc + n],
                            k == 0, k == KC - 1, reads=[wkey, xk(k, c0)], writes=[f"ps{r}"])
                if pending is not None:
                    pending()

                def fin(pk=pk, r=r, n=n, h=h, c0=c0):
                    i = self.rot("st16", 3)
                    st = self.st16[i]
                    self.qk_norm_store(pk, f"ps{r}", n, self.gvec[:, self.off_kn:self.off_kn + 1], "k_norm",
                                       st[:, 0:n], f"st16_{i}")
                    self.dma("sync", self.KT_own[h * 128:(h + 1) * 128, c0:c0 + n], st[:, 0:n],
                             reads=[f"st16_{i}"], writes=[f"KT{h}_{c0}"], key=f"st16_{i}_st")
                    self.kvkeys[h].append(f"KT{h}_{c0}")
                pending = fin
        if pending is not None:
            pending()
        blocks = list(range(base // 128, (base + c.ST[s][1]) // 128))
        GC = c.VGC
        for g in range(c.VG):
            wt, wkey = self.load_w(self.w_v[g], KC * GC, "v")
            for jb in blocks:
                lc = jb * 128 - base
                c0t = self.tile_of(jb * 128, tiles)
                r = self.rot("vps", 2)
                pv = self.ps[4 + r]
                for k in range(KC):
                    self.mm(pv[:, 0:GC], self.xn[:, k * XW + lc:k * XW + lc + 128], wt[:, k * GC:(k + 1) * GC],
                            k == 0, k == KC - 1, reads=[wkey, xk(k, c0t)], writes=[f"ps{4 + r}"])
                i = self.rot("st16", 3)
                st = self.st16[i]
                self.act(st[:, 0:GC], pv[:, 0:GC], AF.Copy, reads=[f"ps{4 + r}"], writes=[f"st16_{i}"])
                for hh in range(GC // 128):
                    h = g * (GC // 128) + hh
                    self.dma("sync", self.V_own[h * 128:(h + 1) * 128, jb * 128:(jb + 1) * 128],
                             st[:, hh * 128:(hh + 1) * 128], reads=[f"st16_{i}"], writes=[f"V{h}_{jb}"], key=f"st16_{i}_st")
                    for h2 in range(g * (GC // 128), (g + 1) * (GC // 128)):
                        self.kvkeys[h2].append(f"V{h}_{jb}")
        for jb in blocks:
            lc = jb * 128 - base
            c0t = self.tile_of(jb * 128, tiles)
            pf = self.ps[6]
            for k in range(KC):
                self.mm(pf[:, 0:H], self.xn[:, k * XW + lc:k * XW + lc + 128], self.wf_bf[:, k * H:(k + 1) * H],
                        k == 0, k == KC - 1, reads=["wf_bf", xk(k, c0t)], writes=["ps6"])
            b = self.rot("tmpb", 2)
            t = self.tmpb[b]
            S.add("vector", lambda e, t=t, pf=pf: e.tensor_tensor(out=t[:, 0:H], in0=pf[:, 0:H], in1=self.bfb[:, 0:H], op=ALU.add),
                  reads=["ps6", "b_f"], writes=[f"tmpb{b}"])
            self.act(t[:, 0:H], t[:, 0:H], AF.Exp, reads=[f"tmpb{b}"], writes=[f"tmpb{b}"], scale=-1.0)
            self.act(t[:, 0:H], t[:, 0:H], AF.Ln, reads=[f"tmpb{b}"], writes=[f"tmpb{b}"], bias=1.0, scale=1.0)
            dst = self.logf[:, jb * H:(jb + 1) * H]
            S.add("vector", lambda e, t=t, dst=dst: e.tensor_scalar(
                out=dst, in0=t[:, 0:H], scalar1=-1.0, scalar2=None, op0=ALU.mult),
                reads=[f"tmpb{b}"], writes=[f"logf{jb}"])

    def tile_of(self, col, tiles):
        for (c0, n) in tiles:
            if c0 <= col < c0 + n:
                return c0
        raise KeyError(col)

    def cumsum_phase(self):
        c = self.cfg
        H = c.H
        S = self.S
        for jb in range(c.NBLK):
            pc = self.ps[6]
            self.mm(pc[:, 0:H], self.tri32[:], self.logf[:, jb * H:(jb + 1) * H], True, jb == 0,
                    reads=["tri32", f"logf{jb}"], writes=["ps6"])
            for i in range(jb):
                self.mm(pc[:, 0:H], self.ones32[:], self.logf[:, i * H:(i + 1) * H], False, i == jb - 1,
                        reads=["ones32", f"logf{i}"], writes=["ps6"])
            S.add("vector", lambda e, pc=pc, jb=jb: e.tensor_copy(out=self.csb[:, jb * H:(jb + 1) * H], in_=pc[:, 0:H]),
                  reads=["ps6"], writes=["csb"])

    def _build_A(self):
        c = self.cfg
        KC, H = c.KC, c.H
        S = self.S
        NL = c.NL_A + c.NL_B
        o = 0
        self.off_an = o
        self.load_vec("a_norm", self.a_norm, c.NL_A * KC, o); o += c.NL_A * KC
        self.off_cv = o
        self.load_vec("a_conv", self.a_conv, c.NL_A * 3 * KC, o); o += c.NL_A * 3 * KC
        self.off_kvn = o
        self.load_vec("kv_norm", self.kv_norm, KC, o); o += KC
        self.off_kn = o
        self.load_vec("k_norm", self.k_norm, 1, o); o += 1
        self.off_ffn = o
        self.load_vec("ffn_norm", self.ffn_norm, NL * KC, o); o += NL * KC
        self.off_bn = o
        self.load_vec("b_norm", self.b_norm, c.NL_B * KC, o); o += c.NL_B * KC
        self.off_qn = o
        self.load_vec("b_q_norm", self.b_q_norm, c.NL_B, o); o += c.NL_B
        assert o <= 16 * KC + 64
        self.bfb = self.sb("bfb", [128, H], F32)
        self.dma("sync", self.bfb[:], self.b_f, [], ["b_f"], "b_f")
        self.tri32 = self.sb("tri32", [128, 128], F32)
        self.dma("sync", self.tri32[:], self.tri32_d, [], ["tri32"], "tri32")
        self.ones32 = self.sb("ones32", [128, 128], F32)
        S.add("vector", lambda e: e.memset(self.ones32[:], 1.0), writes=["ones32"])
        self.wf_bf = self.sb("wf_bf", [128, KC * H], BF16)
        self.dma("gpsimd", self.wf_bf[:], self.w_f, [], ["wf_bf"], "wf_bf")
        self.logf = self.sb("logf", [128, c.NBLK * H], F32)
        self.csb = self.sb("csb", [128, c.NBLK * H], F32)
        self.hkey_kv = lambda m, c0: [f"h{m}_112", f"hpad{m}"] if c0 == 0 else f"h{m}_{c0}"
        self.xnkey_kv = lambda k, c0: f"xn{k}_{c0}"
        for m in range(KC):
            i = self.rot("hs", 3)
            hs = self.hs[i]
            self.dma("sync", hs[:, 0:112], self.xT[m * 128:(m + 1) * 128, 0:112], [], [f"hs{i}"], f"hs{i}_ld")
            self.dma("sync", self.hT[m * 128:(m + 1) * 128, 0:112], hs[:, 0:112], [f"hs{i}"], [f"hpad{m}"], f"hs{i}_st")
        self.kvkeys = {h: [] for h in range(H)}
        for s in range(len(c.ST_A)):
            src, skey = self.xT, self.xkey
            for l in range(c.NL_A):
                self.mixer_a(l, s, src, skey)
                src, skey = self.hT, self.hkey
                self.ffn(l, s, src, skey)
            self.kv_phase(s)
        self.cumsum_phase()

    def _build_B(self):
        c = self.cfg
        KC, H = c.KC, c.H
        S = self.S
        NB = c.NBLK
        self.sel32 = self.sb("sel32", [128, 128], F32)
        self.dma("sync", self.sel32[:], self.sel32_d, [], ["sel32"], "sel32")
        self.trim = self.sb("trim", [128, 128], BF16)
        self.dma("gpsimd", self.trim[:], self.trim_d, [], ["trim"], "trim")
        self.koff_sb = self.sb("koff_sb", [128, NB], F32)
        self.dma("sync", self.koff_sb[:], self.koff, [], ["koff"], "koff")
        self.cneg = self.sb("cneg", [128, H * NB], F32)
        NTH = c.NTILE
        self.cref = self.sb("cref", [128, NTH * H], F32)
        self.biasb = [self.sb(f"biasb{i}", [128, NB], F32) for i in range(4)]
        self.rD = self.sb("rD", [128, 512], F32)
        self.pbuf = [self.sb(f"pbuf{i}", [128, c.TT], BF16) for i in range(4)]
        pz = self.ps[6]
        co3 = self.csb[:].rearrange("p (j h) -> p j h", h=H)
        for h in range(H):
            dd = self.cneg[:, h * NB:(h + 1) * NB]
            S.add("vector", lambda e, dd=dd, h=h: e.tensor_scalar(
                out=dd, in0=co3[:, :, h], scalar1=-1.0, scalar2=None, op0=ALU.mult),
                reads=["csb"], writes=["cneg"])
            S.add("vector", lambda e, dd=dd: e.tensor_tensor(out=dd, in0=dd, in1=self.koff_sb[:], op=ALU.subtract),
                  reads=["cneg", "koff"], writes=["cneg"])
        for t in range(c.NTILE):
            jb = c.PB + (c.TT // 128) * t + (c.TT // 256) - 1
            th = t
            self.mm(pz[:, 0:H], self.sel32[:], self.csb[:, jb * H:(jb + 1) * H], True, True, ["sel32", "csb"], ["ps6"])
            S.add("vector", lambda e, th=th: e.tensor_copy(out=self.cref[:, th * H:(th + 1) * H], in_=pz[:, 0:H]),
                  reads=["ps6"], writes=["cref"])
        src, skey = self.hT, self.hkey
        for l in range(c.NL_B):
            for s in range(len(c.ST_B)):
                self.mixer_b(l, s, src, skey)
                last = l == c.NL_B - 1
                self.ffn(c.NL_A + l, s, self.hT, self.hkey, dst_final=self.outT if last else None)
        allw = [f"out{m}_{c0}" for m in range(KC) for s in range(len(c.ST_B)) for (c0, n) in c.tiles(s)]
        S.add("sync", None, reads=allw)

    def mixer_b(self, l, s, src, srckeyf):
        c = self.cfg
        KC, H, NB, TT = c.KC, c.H, c.NBLK, c.TT
        S = self.S
        XW = c.XW
        base, _ = c.ST[s]
        tiles = c.tiles(s)
        self.norm(src, srckeyf, "b_norm", self.off_bn + l * KC, tiles, base)
        XB = c.XWB
        OQ = H * XB
        OKV = (H + 2) * XB
        KVW = NB * 128
        scale = 1.0 / float(np.sqrt(128.0))
        SUB = TT // 128
        def emit_bias(hh):
            for ti, (c0, n) in enumerate(tiles):
                t = (c0 - c.OWN0) // TT
                bi_ = (hh % 2) * 2 + (ti % 2)
                bb = self.biasb[bi_]
                S.add("vector", lambda e, bb=bb, t=t, hh=hh: e.tensor_scalar(
                    out=bb[:, 0:NB], in0=self.cneg[:, hh * NB:(hh + 1) * NB],
                    scalar1=self.cref[:, t * H + hh:t * H + hh + 1], scalar2=None, op0=ALU.add),
                    reads=["cneg", "cref"], writes=[f"biasb{bi_}"])
        emit_bias(0)
        for h in range(H):
            if h + 1 < H:
                emit_bias(h + 1)
            wt, wkey = self.load_w(self.b_w_q[l * H + h], KC * 128, "q")
            qs = self.rot("qslot", 2)
            kvs = self.rot("kvslot", 2)
            kt = self.big[:, OKV + kvs * 2 * KVW:OKV + kvs * 2 * KVW + KVW]
            vt = self.big[:, OKV + kvs * 2 * KVW + KVW:OKV + (kvs + 1) * 2 * KVW]
            kkey, vkey = f"kt{kvs}", f"vt{kvs}"
            self.dma("sync", kt, self.KT_own[h * 128:(h + 1) * 128, :], self.kvkeys[h], [kkey], kkey)
            self.dma("sync", vt, self.V_own[h * 128:(h + 1) * 128, :], self.kvkeys[h], [vkey], vkey)
            pending = None
            for (c0, n) in tiles:
                lc = c0 - base
                r = self.rot("qps", 2)
                pq = self.ps[r]
                for k in range(KC):
                    self.mm(pq[:, 0:n], wt[:, k * 128:(k + 1) * 128], self.xn[:, k * XW + lc:k * XW + lc + n],
                            k == 0, k == KC - 1, reads=[wkey, f"xn{k}_{c0}"], writes=[f"ps{r}"])
                if pending is not None:
                    pending()

                def fin(pq=pq, r=r, n=n, lc=lc, c0=c0, qs=qs):
                    self.qk_norm_store(pq, f"ps{r}", n, self.gvec[:, self.off_qn + l:self.off_qn + l + 1], "b_q_norm",
                                       self.big[:, OQ + qs * XB + lc:OQ + qs * XB + lc + n], f"q{qs}_{c0}")
                pending = fin
            if pending is not None:
                pending()
            for ti, (c0, n) in enumerate(tiles):
                lc = c0 - base
                t = (c0 - c.OWN0) // TT
                qap = self.big[:, OQ + qs * XB + lc:OQ + qs * XB + lc + n]
                qkey = f"q{qs}_{c0}"
                bi_ = (h % 2) * 2 + (ti % 2)
                bb = self.biasb[bi_]
                bbkey = f"biasb{bi_}"
                ND = c.PB + SUB * t
                blks = [(j, 0) for j in range(ND)] + [(ND + i, i) for i in range(SUB)]
                ar = self.rot("attps", 2)
                pO, pD = self.ps[2 + ar], self.ps[4 + ar]
                okey, dkey = f"ps{2 + ar}", f"ps{4 + ar}"
                nb = len(blks)

                def issue_S(bi):
                    gb, i = blks[bi]
                    qa = 128 * i
                    sr = (6, 7, 1, 0)[self.rot("sps", 4)]
                    pS = self.ps[sr]
                    skey_ = f"ps{sr}"
                    self.mm(pS[:, qa:n], kt[:, gb * 128:(gb + 1) * 128], qap[:, qa:n], True, True,
                            reads=[kkey, qkey], writes=[skey_])
                    return pS, skey_

                def issue_rest(bi, pS, skey_):
                    gb, i = blks[bi]
                    qa = 128 * i
                    pi = self.rot("pbuf", 4)
                    P = self.pbuf[pi]
                    pkey = f"pbuf{pi}"
                    self.act(P[:, qa:n], pS[:, qa:n], AF.Exp, reads=[skey_, bbkey], writes=[pkey],
                             bias=bb[:, gb:gb + 1], scale=scale)
                    if gb >= ND:
                        S.add("gpsimd", lambda e, P=P, qa=qa: e.tensor_tensor(
                            out=P[:, qa:qa + 128], in0=P[:, qa:qa + 128], in1=self.trim[:], op=ALU.mult),
                            reads=[pkey, "trim"], writes=[pkey])
                    self.mm(pO[:, qa:n], vt[:, gb * 128:(gb + 1) * 128], P[:, qa:n], bi == 0, bi == nb - 1,
                            reads=[vkey, pkey], writes=[okey])
                    self.mm(pD[:, qa:n], self.ones_bf[:], P[:, qa:n], bi == 0, bi == nb - 1,
                            reads=["ones_bf", pkey], writes=[dkey])

                LA = 3
                sq_ = [issue_S(i) for i in range(min(LA, nb))]
                for bi in range(nb):
                    if bi + LA < nb:
                        sq_.append(issue_S(bi + LA))
                    issue_rest(bi, *sq_[bi])
                S.add("vector", lambda e, pD=pD, n=n: e.reciprocal(out=self.rD[:, 0:n], in_=pD[:, 0:n]),
                      reads=[dkey], writes=["rD"])
                dst = self.big[:, h * XB + lc:h * XB + lc + n]
                S.add("vector", lambda e, dst=dst, pO=pO, n=n: e.tensor_tensor(
                    out=dst, in0=pO[:, 0:n], in1=self.rD[:, 0:n], op=ALU.mult),
                    reads=[okey, "rD"], writes=[f"big{h}_{c0}"])
        for m in range(KC):
            wt, wkey = self.load_w(self.b_w_o[l * KC + m], KC * 128, "o")
            for (c0, n) in tiles:
                lc = c0 - base
                r = self.rot("kps", 2)
                po = self.ps[r]
                for k in range(KC):
                    self.mm(po[:, 0:n], wt[:, k * 128:(k + 1) * 128], self.big[:, k * XB + lc:k * XB + lc + n],
                            k == 0, k == KC - 1, reads=[wkey, f"big{k}_{c0}"], writes=[f"ps{r}"])
                self.resid(po, f"ps{r}", src, srckeyf, self.hT, self.hkey, m, c0, n)


def _consts():
    i = np.arange(128)
    tri = (i[:, None] <= i[None, :]).astype(np.float32)
    sel = np.zeros((128, 128), np.float32)
    sel[127, :] = 1.0
    return tri, sel


def run_fused(cfg, x, meta, a_norm, a_w_in, a_conv, a_w_out, kv_norm, w_kv, k_norm, w_f, b_f,
              b_norm, b_w_q, b_q_norm, b_w_o, ffn_norm, ffn_w_gu, ffn_w_down, n_cores=8):
    c = cfg
    D, F, KC, FC, H = c.D, c.F, c.KC, c.FC, c.H
    B = n_cores // 2
    NM = c.NMAIN
    NB = c.NBLK
    f32 = np.float32
    x = np.asarray(x, f32)
    meta = np.asarray(meta, f32)
    tri, sel = _consts()
    LA, LB = c.NL_A, c.NL_B
    shared = {
        "a_norm_T": vecT(a_norm),
        "a_w_in_b": np.concatenate([blk(np.asarray(a_w_in[l], f32)) for l in range(LA)], 0),
        "a_conv_T": vecT(np.asarray(a_conv, f32).reshape(LA * 3, D)),
        "a_w_out_b": np.concatenate([blk(np.asarray(a_w_out[l], f32)) for l in range(LA)], 0),
        "kv_norm_T": vecT(kv_norm),
        "w_k_b": blk(np.asarray(w_kv, f32)[:, :D]),
        "w_v_b": blk(np.asarray(w_kv, f32)[:, D:], c.VGC),
        "k_norm_T": np.ascontiguousarray(np.asarray(k_norm, f32).reshape(128, 1)),
        "w_f_l": np.ascontiguousarray(np.asarray(w_f, f32).reshape(KC, 128, H).transpose(1, 0, 2).reshape(128, KC * H)),
        "b_f_b": np.ascontiguousarray(np.broadcast_to(np.asarray(b_f, f32)[None, :], (128, H))),
        "tri32": tri,
        "b_norm_T": vecT(b_norm),
        "b_w_q_b": np.concatenate([blk(np.asarray(b_w_q[l], f32)) for l in range(LB)], 0),
        "b_q_norm_T": np.ascontiguousarray(np.asarray(b_q_norm, f32).T),
        "b_w_o_b": np.concatenate([blk(np.asarray(b_w_o[l], f32)) for l in range(LB)], 0),
        "sel32": sel,
        "trimask": tri,
        "ffn_norm_T": vecT(np.asarray(ffn_norm, f32)),
        "ffn_w_gu_b": np.concatenate([blk(np.asarray(ffn_w_gu[l], f32)) for l in range(LA + LB)], 0),
        "ffn_w_down_b": np.concatenate([blk(np.asarray(ffn_w_down[l], f32)) for l in range(LA + LB)], 0),
    }
    in_maps = []
    for core in range(n_cores):
        b, half = core // 2, core % 2
        xT = np.zeros((D, c.NCOL), f32)
        koff = np.zeros((128, NB), f32)
        if half == 0:
            xT[:, c.OWN0 - 16:c.OWN0] = meta.T
            xT[:, c.OWN0:] = x[b, 0:NM].T
            koff[:, :c.PB - 1] = MASKV
            koff[:112, c.PB - 1] = MASKV
        else:
            xT[:, 112:128] = meta.T
            xT[:, 128:] = x[b].T
            koff[:112, 0] = MASKV
        m = dict(shared)
        m["xT"] = xT
        m["koff"] = koff
        in_maps.append(m)
    nc = Prog(c).build()
    res = run_bass_kernel_spmd(nc, in_maps, core_ids=list(range(n_cores))).results
    out = np.empty((B, 2 * NM, D), f32)
    for core in range(n_cores):
        b, half = core // 2, core % 2
        out[b, half * NM:(half + 1) * NM] = np.asarray(res[core]["outT"]).T
    return out


def kernel(**inputs):
    cfg = Cfg()
    return run_fused(cfg, **inputs)
```

```python
import contextlib
import numpy as np
import ml_dtypes
import concourse.bass as bass
import concourse.mybir as mybir
from concourse.bass_utils import run_bass_kernel_spmd

F32 = mybir.dt.float32
BF16 = mybir.dt.bfloat16
AF = mybir.ActivationFunctionType
ALU = mybir.AluOpType

ENGS = ("sync", "scalar", "vector", "gpsimd", "tensor")
SEM_ROLL = 24000
EPS = 1e-6
MASKV = 60000.0


class _Op:
    __slots__ = ("eng", "fn", "deps", "is_dma", "key", "sem", "semval", "signal", "inc")

    def __init__(self, eng, fn, is_dma, key, inc=16):
        self.inc = inc
        self.eng = eng
        self.fn = fn
        self.deps = []
        self.is_dma = is_dma
        self.key = key
        self.sem = None
        self.semval = 0
        self.signal = False


class Sched:
    def __init__(self, nc, stack):
        self.nc = nc
        self.stack = stack
        self.q = {e: [] for e in ENGS}
        self.lastw = {}
        self.readers = {}
        self.nsem = 0
        self.nops = 0

    def _newsem(self, name):
        self.nsem += 1
        return self.stack.enter_context(self.nc.semaphore(f"s{self.nsem}_{name}"))

    def add(self, eng, fn, reads=(), writes=(), dma_key=None, inc=16):
        op = _Op(eng, fn, dma_key is not None, dma_key, inc)
        self.nops += 1
        deps = set()
        for k in reads:
            w = self.lastw.get(k)
            if w is not None:
                deps.add(w)
        for k in writes:
            w = self.lastw.get(k)
            if w is not None:
                deps.add(w)
            for r in self.readers.get(k, ()):
                deps.add(r)
        for d in deps:
            if d is op:
                continue
            if d.eng == "tensor" and eng == "tensor" and not d.is_dma:
                continue
            op.deps.append(d)
            d.signal = True
        for k in writes:
            self.lastw[k] = op
            self.readers[k] = []
        for k in reads:
            self.readers.setdefault(k, []).append(op)
        self.q[eng].append(op)
        return op

    def emit(self):
        nc = self.nc
        dma_sems = {}
        dma_cnt = {}
        for e in ENGS:
            sem = None
            cnt = 0
            for op in self.q[e]:
                if op.is_dma:
                    if op.key not in dma_sems:
                        dma_sems[op.key] = self._newsem("d")
                        dma_cnt[op.key] = 0
                    dma_cnt[op.key] += op.inc
                    op.sem = dma_sems[op.key]
                    op.semval = dma_cnt[op.key]
                elif op.signal:
                    if sem is None or cnt >= SEM_ROLL:
                        sem = self._newsem(e)
                        cnt = 0
                    cnt += 1
                    op.sem = sem
                    op.semval = cnt
        q = self.q

        def run(e, eng):
            waited = {}
            for op in q[e]:
                need = {}
                for d in op.deps:
                    sid = id(d.sem)
                    if waited.get(sid, 0) >= d.semval:
                        continue
                    if sid not in need or need[sid][1] < d.semval:
                        need[sid] = (d.sem, d.semval)
                for sid, (s, v) in need.items():
                    eng.wait_ge(s, v)
                    waited[sid] = v
                if op.fn is None:
                    continue
                ins = op.fn(eng)
                if op.is_dma:
                    if op.inc == 16:
                        ins.then_inc(op.sem, 16)
                    else:
                        ins.then_inc(op.sem)
                elif op.signal:
                    ins.then_inc(op.sem, 1)

        with nc.Block() as block:
            @block.sync
            def _(eng):
                run("sync", eng)

            @block.scalar
            def _(eng):
                run("scalar", eng)

            @block.vector
            def _(eng):
                run("vector", eng)

            @block.gpsimd
            def _(eng):
                run("gpsimd", eng)

            @block.tensor
            def _(eng):
                run("tensor", eng)


class Cfg:
    def __init__(self, D=2048, F=5632, NMAIN=2048, TT=512, NL_A=2, NL_B=2):
        self.D = D
        self.F = F
        self.KC = D // 128
        self.FC = F // 128
        self.H = D // 128
        self.NMAIN = NMAIN
        self.TT = TT
        self.NTILE = NMAIN // TT
        self.NCOL = 128 + 2 * NMAIN
        self.NBLK = self.NCOL // 128
        self.OWN0 = 128 + NMAIN
        self.PB = self.OWN0 // 128
        self.NL_A = NL_A
        self.NL_B = NL_B
        self.SW = 2 * TT
        self.XW = 128 + self.SW
        self.XWB = self.SW
        self.ST_A = [(0, 128 + self.SW)] + [(128 + self.SW * i, self.SW) for i in range(1, 2 * NMAIN // self.SW)]
        self.ST_B = [(self.OWN0 + self.SW * i, self.SW) for i in range(NMAIN // self.SW)]
        self.ST = self.ST_A
        self.FSPLIT = 2 if self.FC % 2 == 0 else 1
        self.FH = self.FC // self.FSPLIT
        self.VGC = min(256, D)
        self.VG = D // self.VGC

    def set_mode(self, mode):
        self.ST = self.ST_A if mode == "A" else self.ST_B

    def tiles(self, s, with_prefix=True, kvmode=False):
        base, n = self.ST[s]
        out = []
        c = base
        if s == 0 and base == 0:
            if kvmode:
                out.append((0, 128))
            elif with_prefix:
                out.append((112, 16))
            c = 128
        while c < base + n:
            out.append((c, self.TT))
            c += self.TT
        return out


def blk(W, gc=128):
    Din, Dout = W.shape
    return np.ascontiguousarray(
        W.reshape(Din // 128, 128, Dout // gc, gc).transpose(2, 1, 0, 3).reshape(Dout // gc, 128, (Din // 128) * gc))


def vecT(g):
    g = np.asarray(g, np.float32)
    kc = g.shape[-1] // 128
    a = g.reshape(-1, kc, 128)
    return np.ascontiguousarray(a.transpose(2, 0, 1).reshape(128, -1))


class Prog:
    def __init__(self, cfg):
        self.cfg = cfg
        cfg.set_mode("A")
        self.mode = "R"
        self.nc = bass.Bass("TRN2", target_bir_lowering=False)
        self.stack = contextlib.ExitStack()
        self.S = Sched(self.nc, self.stack)
        self.wi = 0
        self.cnt = {}

    def din(self, name, shape, dt=F32):
        return self.nc.dram_tensor(name, list(shape), dt, kind="ExternalInput").ap()

    def dout(self, name, shape, dt=F32):
        return self.nc.dram_tensor(name, list(shape), dt, kind="ExternalOutput").ap()

    def sb(self, name, shape, dt):
        return self.stack.enter_context(self.nc.sbuf_tensor("sb_" + name, list(shape), dt))

    def rot(self, name, n):
        i = self.cnt.get(name, 0)
        self.cnt[name] = i + 1
        return i % n

    def dma(self, eng, out, in_, reads, writes, key):
        return self.S.add(eng, lambda e: e.dma_start(out=out, in_=in_), reads=reads, writes=writes, dma_key=key)

    def mm(self, out, lhsT, rhs, start, stop, reads, writes):
        return self.S.add("tensor", lambda e: e.matmul(out, lhsT=lhsT, rhs=rhs, start=start, stop=stop),
                          reads=reads, writes=writes)

    def act(self, out, in_, func, reads, writes, bias=None, scale=None):
        kw = {}
        if bias is not None:
            kw["bias"] = bias
        if scale is not None:
            kw["scale"] = scale
        return self.S.add("scalar", lambda e: e.activation(out=out, in_=in_, func=func, **kw), reads=reads, writes=writes)

    def build(self):
        with self.stack:
            self._build()
            self.S.emit()
        return self.nc

    def _build(self):
        c = self.cfg
        nc = self.nc
        KC, FC, H, D, F = c.KC, c.FC, c.H, c.D, c.F
        NL = c.NL_A + c.NL_B
        self.xT = self.din("xT", [D, c.NCOL])
        self.hT = nc.dram_tensor("hT_scr", [D, c.NCOL], F32).ap()
        self.outT = self.dout("outT", [D, c.NMAIN])
        self.KT_own = nc.dram_tensor("kt_scr", [H * 128, c.NCOL], BF16).ap()
        self.V_own = nc.dram_tensor("v_scr", [H * 128, c.NBLK * 128], BF16).ap()
        self.a_norm = self.din("a_norm_T", [128, c.NL_A * KC])
        self.a_w_in = self.din("a_w_in_b", [c.NL_A * 3 * KC, 128, KC * 128])
        self.a_conv = self.din("a_conv_T", [128, c.NL_A * 3 * KC])
        self.a_w_out = self.din("a_w_out_b", [c.NL_A * KC, 128, KC * 128])
        self.kv_norm = self.din("kv_norm_T", [128, KC])
        self.w_k = self.din("w_k_b", [H, 128, KC * 128])
        self.w_v = self.din("w_v_b", [c.VG, 128, KC * c.VGC])
        self.k_norm = self.din("k_norm_T", [128, 1])
        self.w_f = self.din("w_f_l", [128, KC * H])
        self.b_f = self.din("b_f_b", [128, H])
        self.tri32_d = self.din("tri32", [128, 128])
        self.b_norm = self.din("b_norm_T", [128, c.NL_B * KC])
        self.b_w_q = self.din("b_w_q_b", [c.NL_B * H, 128, KC * 128])
        self.b_q_norm = self.din("b_q_norm_T", [128, c.NL_B])
        self.b_w_o = self.din("b_w_o_b", [c.NL_B * KC, 128, KC * 128])
        self.koff = self.din("koff", [128, c.NBLK])
        self.sel32_d = self.din("sel32", [128, 128])
        self.trim_d = self.din("trimask", [128, 128])
        self.ffn_norm = self.din("ffn_norm_T", [128, NL * KC])
        self.ffn_w_gu = self.din("ffn_w_gu_b", [NL * 2 * FC, 128, KC * 128])
        self.ffn_w_down = self.din("ffn_w_down_b", [NL * KC, 128, FC * 128])

        TT = c.TT
        self.xn = self.sb("xn", [128, KC * c.XW], BF16)
        bigw = max(c.FH * c.XW, KC * c.XW, (H + 2) * c.XWB + 4 * c.NBLK * 128)
        self.big = self.sb("big", [128, bigw], BF16)
        self.WSLOT = 6
        self.WW = max(KC * c.VGC, c.FH * 128, KC * 128)
        self.wbuf = [self.sb(f"wbuf{i}", [128, self.WW], BF16) for i in range(self.WSLOT)]
        self.NXS = 4
        self.xs = [self.sb(f"xs{i}", [128, TT], F32) for i in range(self.NXS)]
        self.sqb = [self.sb(f"sqb{i}", [128, TT], BF16) for i in range(2)]
        self.rstd = self.sb("rstd", [128, TT], F32)
        self.tmpa = [self.sb(f"tmpa{i}", [128, TT], F32) for i in range(2)]
        self.tmpb = [self.sb(f"tmpb{i}", [128, TT], F32) for i in range(2)]
        self.ub = [self.sb(f"ub{i}", [128, TT + 2], F32) for i in range(2)]
        self.acc = [self.sb(f"acc{i}", [128, TT], F32) for i in range(2)]
        self.hs = [self.sb(f"hs{i}", [128, TT], F32) for i in range(3)]
        self.st16 = [self.sb(f"st16_{i}", [128, TT], BF16) for i in range(2)]
        self.uh = self.sb("uh", [128, 2 * KC * 2], F32)
        self.ones_bf = self.sb("ones_bf", [128, 128], BF16)
        self.gvec = self.sb("gvec", [128, 16 * KC + 64], F32)
        self.ps = [self.stack.enter_context(nc.psum_tensor(f"ps{i}", [128, 512], F32)) for i in range(8)]
        self.S.add("vector", lambda e: e.memset(self.ones_bf[:], 1.0), writes=["ones_bf"])
        self._build_A()
        c.set_mode("B")
        self._build_B()

    def load_vec(self, name, dram, ncols, off):
        dst = self.gvec[:, off:off + ncols]
        self.dma("sync", dst, dram, reads=[], writes=[name], key=name)
        return dst

    def load_w(self, dram_ap, width, tag):
        i = self.wi % self.WSLOT
        self.wi += 1
        key = f"wbuf{i}"
        dst = self.wbuf[i][:, 0:width]
        self.dma("gpsimd", dst, dram_ap, reads=[], writes=[key], key=key)
        return self.wbuf[i], key

    def hkey(self, m, c0):
        return f"h{m}_{c0}"

    def xkey(self, m, c0):
        return f"x{m}_{c0}"

    def norm(self, src, srckeyf, gname, goff, tiles, base, zero_first=None):
        c = self.cfg
        KC = c.KC
        S = self.S
        for (c0, n) in tiles:
            lc = c0 - base
            ss = self.ps[7]
            for k in range(KC):
                i = self.rot("xs", self.NXS)
                xs = self.xs[i]
                rk = srckeyf(k, c0)
                rk = rk if isinstance(rk, list) else [rk]
                self.dma("sync", xs[:, 0:n], src[k * 128:(k + 1) * 128, c0:c0 + n],
                         reads=rk, writes=[f"xs{i}"], key=f"xs{i}")
                j = self.rot("sqb", 2)
                sq = self.sqb[j]
                self.act(sq[:, 0:n], xs[:, 0:n], AF.Square, reads=[f"xs{i}"], writes=[f"sqb{j}"])
                self.mm(ss[:, 0:n], self.ones_bf[:], sq[:, 0:n], k == 0, k == KC - 1,
                        reads=[f"sqb{j}", "ones_bf"], writes=["ps7"])
                g = self.gvec[:, goff + k:goff + k + 1]
                dst = self.xn[:, k * c.XW + lc:k * c.XW + lc + n]
                S.add("vector", lambda e, dst=dst, xs=xs, g=g, n=n: e.tensor_scalar(
                    out=dst, in0=xs[:, 0:n], scalar1=g, scalar2=None, op0=ALU.mult),
                    reads=[f"xs{i}", gname], writes=[f"xn{k}_{c0}"])
            self.act(self.rstd[:, 0:n], ss[:, 0:n], AF.Ln, reads=["ps7"], writes=["rstd"],
                     bias=EPS, scale=1.0 / c.D)
            self.act(self.rstd[:, 0:n], self.rstd[:, 0:n], AF.Exp, reads=["rstd"], writes=["rstd"], scale=-0.5)
            for k in range(KC):
                dst = self.xn[:, k * c.XW + lc:k * c.XW + lc + n]
                eng = "vector" if k % 2 == 0 else "gpsimd"
                S.add(eng, lambda e, dst=dst, n=n: e.tensor_tensor(
                    out=dst, in0=dst, in1=self.rstd[:, 0:n], op=ALU.mult),
                    reads=[f"xn{k}_{c0}", "rstd"], writes=[f"xn{k}_{c0}"])

    def resid(self, pst, pkey, src, srckeyf, dst, dstkeyf, m, c0, n, dst_c0=None):
        S = self.S
        i = self.rot("hs", 3)
        hs = self.hs[i]
        self.dma("sync", hs[:, 0:n], src[m * 128:(m + 1) * 128, c0:c0 + n],
                 reads=[srckeyf(m, c0)], writes=[f"hs{i}"], key=f"hs{i}_ld")
        S.add("vector", lambda e: e.tensor_tensor(out=hs[:, 0:n], in0=hs[:, 0:n], in1=pst[:, 0:n], op=ALU.add),
              reads=[f"hs{i}", pkey], writes=[f"hs{i}"])
        dc = c0 if dst_c0 is None else dst_c0
        self.dma("sync", dst[m * 128:(m + 1) * 128, dc:dc + n], hs[:, 0:n],
                 reads=[f"hs{i}"], writes=[dstkeyf(m, c0)], key=f"hs{i}_st")

    def hkey(self, m, c0):
        return f"h{m}_{c0}"

    def xkey(self, m, c0):
        return f"x{m}_{c0}"

    def resid(self, pst, pkey, src, srckeyf, dst, dstkeyf, m, c0, n, dst_c0=None):
        S = self.S
        i = self.rot("hs", 3)
        hs = self.hs[i]
        self.dma("sync", hs[:, 0:n], src[m * 128:(m + 1) * 128, c0:c0 + n],
                 reads=[srckeyf(m, c0)], writes=[f"hs{i}"], key=f"hs{i}_ld")
        S.add("vector", lambda e: e.tensor_tensor(out=hs[:, 0:n], in0=hs[:, 0:n], in1=pst[:, 0:n], op=ALU.add),
              reads=[f"hs{i}", pkey], writes=[f"hs{i}"])
        dc = c0 if dst_c0 is None else dst_c0
        self.dma("sync", dst[m * 128:(m + 1) * 128, dc:dc + n], hs[:, 0:n],
                 reads=[f"hs{i}"], writes=[dstkeyf(m, c0)], key=f"hs{i}_st")

    def ffn(self, l, s, src, srckeyf, dst_final=None):
        c = self.cfg
        KC, FC, FH = c.KC, c.FC, c.FH
        S = self.S
        base, _ = c.ST[s]
        tiles = c.tiles(s)

        def loadf(f):
            return (self.load_w(self.ffn_w_gu[l * 2 * FC + f], KC * 128, "g"),
                    self.load_w(self.ffn_w_gu[l * 2 * FC + FC + f], KC * 128, "u"))
        pre = {0: loadf(0)}
        self.norm(src, srckeyf, "ffn_norm", self.off_ffn + l * KC, tiles, base)
        XW = c.XW
        cur_src, cur_key = src, srckeyf
        for fh in range(c.FSPLIT):
            for fi in range(FH):
                f = fh * FH + fi
                (wt, wkey), (wt2, wkey2) = pre.pop(f) if f in pre else loadf(f)
                for (c0, n) in tiles:
                    lc = c0 - base
                    r = self.rot("ffnps", 2)
                    pg, pu = self.ps[2 * r], self.ps[2 * r + 1]
                    for k in range(KC):
                        self.mm(pg[:, 0:n], wt[:, k * 128:(k + 1) * 128], self.xn[:, k * XW + lc:k * XW + lc + n],
                                k == 0, k == KC - 1, reads=[wkey, f"xn{k}_{c0}"], writes=[f"ps{2 * r}"])
                    for k in range(KC):
                        self.mm(pu[:, 0:n], wt2[:, k * 128:(k + 1) * 128], self.xn[:, k * XW + lc:k * XW + lc + n],
                                k == 0, k == KC - 1, reads=[wkey2, f"xn{k}_{c0}"], writes=[f"ps{2 * r + 1}"])
                    j = self.rot("tmpa", 2)
                    sg = self.tmpa[j]
                    self.act(sg[:, 0:n], pg[:, 0:n], AF.Silu, reads=[f"ps{2 * r}"], writes=[f"tmpa{j}"])
                    dst = self.big[:, fi * XW + lc:fi * XW + lc + n]
                    S.add("vector", lambda e, dst=dst, sg=sg, pu=pu, n=n: e.tensor_tensor(
                        out=dst, in0=sg[:, 0:n], in1=pu[:, 0:n], op=ALU.mult),
                        reads=[f"tmpa{j}", f"ps{2 * r + 1}"], writes=[f"big{fi}_{c0}"])
            last = fh == c.FSPLIT - 1
            for m in range(KC):
                wt, wkey = self.load_w(self.ffn_w_down[l * KC + m][:, fh * FH * 128:(fh + 1) * FH * 128], FH * 128, "d")
                for (c0, n) in tiles:
                    lc = c0 - base
                    r = self.rot("dps", 2)
                    po = self.ps[4 + r]
                    for fi in range(FH):
                        self.mm(po[:, 0:n], wt[:, fi * 128:(fi + 1) * 128], self.big[:, fi * XW + lc:fi * XW + lc + n],
                                fi == 0, fi == FH - 1, reads=[wkey, f"big{fi}_{c0}"], writes=[f"ps{4 + r}"])
                    if last and dst_final is not None:
                        self.resid(po, f"ps{4 + r}", cur_src, cur_key, dst_final, lambda m, c0: f"out{m}_{c0}",
                                   m, c0, n, dst_c0=c0 - c.OWN0)
                    else:
                        self.resid(po, f"ps{4 + r}", cur_src, cur_key, self.hT, self.hkey, m, c0, n)
            cur_src, cur_key = self.hT, self.hkey

    def mixer_a(self, l, s, src, srckeyf):
        c = self.cfg
        KC = c.KC
        S = self.S
        XW = c.XW
        base, _ = c.ST[s]
        tiles = c.tiles(s)

        def loadm(m):
            return [self.load_w(self.a_w_in[(l * 3 + part) * KC + m], KC * 128, "in") for part in range(3)]
        nxt = loadm(0)
        self.norm(src, srckeyf, "a_norm", self.off_an + l * KC, tiles, base)
        uo = l * KC * 2
        if s == 0:
            S.add("vector", lambda e, uo=uo: e.memset(self.uh[:, uo:uo + 2 * KC], 0.0), writes=[f"uh{l}_{m}" for m in range(KC)])
        for m in range(KC):
            wts = nxt
            if m + 1 < KC:
                nxt = loadm(m + 1)
            cw = [self.gvec[:, self.off_cv + (l * 3 + j) * KC + m:self.off_cv + (l * 3 + j) * KC + m + 1] for j in range(3)]
            for (c0, n) in tiles:
                lc = c0 - base
                r = self.rot("aps", 2)
                pp = [self.ps[3 * r + q] for q in range(3)]
                for part in range(3):
                    wt, wkey = wts[part]
                    for k in range(KC):
                        self.mm(pp[part][:, 0:n], wt[:, k * 128:(k + 1) * 128], self.xn[:, k * XW + lc:k * XW + lc + n],
                                k == 0, k == KC - 1, reads=[wkey, f"xn{k}_{c0}"], writes=[f"ps{3 * r + part}"])
                pb, pc, ph = pp
                j = self.rot("tmpa", 2)
                csb = self.tmpa[j]
                self.act(csb[:, 0:n], pc[:, 0:n], AF.Copy, reads=[f"ps{3 * r + 1}"], writes=[f"tmpa{j}"])
                ui = self.rot("ub", 2)
                ub = self.ub[ui]
                S.add("scalar", lambda e, ub=ub, m=m, uo=uo: e.activation(out=ub[:, 0:2], in_=self.uh[:, uo + 2 * m:uo + 2 * m + 2], func=AF.Copy),
                      reads=[f"uh{l}_{m}"], writes=[f"ub{ui}"])
                S.add("vector", lambda e, ub=ub, csb=csb, ph=ph, n=n: e.tensor_tensor(
                    out=ub[:, 2:2 + n], in0=csb[:, 0:n], in1=ph[:, 0:n], op=ALU.mult),
                    reads=[f"tmpa{j}", f"ps{3 * r + 2}"], writes=[f"ub{ui}"])
                S.add("scalar", lambda e, ub=ub, m=m, n=n, uo=uo: e.activation(out=self.uh[:, uo + 2 * m:uo + 2 * m + 2], in_=ub[:, n:n + 2], func=AF.Copy),
                      reads=[f"ub{ui}"], writes=[f"uh{l}_{m}"])
                ai = self.rot("acc", 2)
                acc = self.acc[ai]
                S.add("scalar", lambda e, acc=acc, ub=ub, n=n, w=cw[2]: e.activation(
                    out=acc[:, 0:n], in_=ub[:, 2:2 + n], func=AF.Copy, scale=w),
                    reads=[f"ub{ui}", "a_conv"], writes=[f"acc{ai}"])
                S.add("vector", lambda e, acc=acc, ub=ub, n=n, w=cw[1]: e.scalar_tensor_tensor(
                    out=acc[:, 0:n], in0=ub[:, 1:1 + n], scalar=w, in1=acc[:, 0:n], op0=ALU.mult, op1=ALU.add),
                    reads=[f"ub{ui}", f"acc{ai}", "a_conv"], writes=[f"acc{ai}"])
                S.add("vector", lambda e, acc=acc, ub=ub, n=n, w=cw[0]: e.scalar_tensor_tensor(
                    out=acc[:, 0:n], in0=ub[:, 0:n], scalar=w, in1=acc[:, 0:n], op0=ALU.mult, op1=ALU.add),
                    reads=[f"ub{ui}", f"acc{ai}", "a_conv"], writes=[f"acc{ai}"])
                dst = self.big[:, m * XW + lc:m * XW + lc + n]
                S.add("vector", lambda e, dst=dst, acc=acc, pb=pb, n=n: e.tensor_tensor(
                    out=dst, in0=acc[:, 0:n], in1=pb[:, 0:n], op=ALU.mult),
                    reads=[f"acc{ai}", f"ps{3 * r}"], writes=[f"big{m}_{c0}"])
        for m in range(KC):
            wt, wkey = self.load_w(self.a_w_out[l * KC + m], KC * 128, "o")
            for (c0, n) in tiles:
                lc = c0 - base
                r = self.rot("ops", 2)
                po = self.ps[6 + r]
                for k in range(KC):
                    self.mm(po[:, 0:n], wt[:, k * 128:(k + 1) * 128], self.big[:, k * XW + lc:k * XW + lc + n],
                            k == 0, k == KC - 1, reads=[wkey, f"big{k}_{c0}"], writes=[f"ps{6 + r}"])
                self.resid(po, f"ps{6 + r}", src, srckeyf, self.hT, self.hkey, m, c0, n)

    def qk_norm_store(self, pk, pkkey, n, gap, gname, dst, dstkey):
        S = self.S
        j = self.rot("tmpa", 2)
        raw = self.tmpa[j]
        self.act(raw[:, 0:n], pk[:, 0:n], AF.Copy, reads=[pkkey], writes=[f"tmpa{j}"])
        q = self.rot("sqb", 2)
        sq = self.sqb[q]
        self.act(sq[:, 0:n], pk[:, 0:n], AF.Square, reads=[pkkey], writes=[f"sqb{q}"])
        r = self.rot("ssps", 2)
        pss = self.ps[2 + r]
        self.mm(pss[:, 0:n], self.ones_bf[:], sq[:, 0:n], True, True, reads=[f"sqb{q}", "ones_bf"], writes=[f"ps{2 + r}"])
        b = self.rot("tmpb", 2)
        t = self.tmpb[b]
        self.act(t[:, 0:n], pss[:, 0:n], AF.Ln, reads=[f"ps{2 + r}"], writes=[f"tmpb{b}"], bias=EPS, scale=1.0 / 128)
        self.act(t[:, 0:n], t[:, 0:n], AF.Exp, reads=[f"tmpb{b}"], writes=[f"tmpb{b}"], scale=-0.5)
        S.add("vector", lambda e: e.scalar_tensor_tensor(out=dst, in0=raw[:, 0:n], scalar=gap, in1=t[:, 0:n],
                                                         op0=ALU.mult, op1=ALU.mult),
              reads=[f"tmpa{j}", f"tmpb{b}", gname], writes=[dstkey])

    def kv_phase(self, s):
        c = self.cfg
        KC, H = c.KC, c.H
        S = self.S
        XW = c.XW
        base, _ = c.ST[s]
        tiles = c.tiles(s, kvmode=True)
        self.norm(self.hT, self.hkey_kv, "kv_norm", self.off_kvn, tiles, base)
        xk = self.xnkey_kv
        pending = None
        for h in range(H):
            wt, wkey = self.load_w(self.w_k[h], KC * 128, "k")
            for (c0, n) in tiles:
                lc = c0 - base
                r = self.rot("kps", 2)
                pk = self.ps[r]
                for k in range(KC):
                    self.mm(pk[:, 0:n], wt[:, k * 128:(k + 1) * 128], self.xn[:, k * XW + lc:k * XW + lc + n],
                            k == 0, k == KC - 1, reads=[wkey, xk(k, c0)], writes=[f"ps{r}"])
                if pending is not None:
                    pending()

                def fin(pk=pk, r=r, n=n, h=h, c0=c0):
                    i = self.rot("st16", 2)
                    st = self.st16[i]
                    self.qk_norm_store(pk, f"ps{r}", n, self.gvec[:, self.off_kn:self.off_kn + 1], "k_norm",
                                       st[:, 0:n], f"st16_{i}")
                    self.dma("sync", self.KT_own[h * 128:(h + 1) * 128, c0:c0 + n], st[:, 0:n],
                             reads=[f"st16_{i}"], writes=[f"KT{h}_{c0}"], key=f"st16_{i}_st")
                    self.kvkeys[h].append(f"KT{h}_{c0}")
                pending = fin
        if pending is not None:
            pending()
        blocks = list(range(base // 128, (base + c.ST[s][1]) // 128))
        GC = c.VGC
        for g in range(c.VG):
            wt, wkey = self.load_w(self.w_v[g], KC * GC, "v")
            for jb in blocks:
                lc = jb * 128 - base
                c0t = self.tile_of(jb * 128, tiles)
                r = self.rot("vps", 2)
                pv = self.ps[4 + r]
                for k in range(KC):
                    self.mm(pv[:, 0:GC], self.xn[:, k * XW + lc:k * XW + lc + 128], wt[:, k * GC:(k + 1) * GC],
                            k == 0, k == KC - 1, reads=[wkey, xk(k, c0t)], writes=[f"ps{4 + r}"])
                i = self.rot("st16", 2)
                st = self.st16[i]
                self.act(st[:, 0:GC], pv[:, 0:GC], AF.Copy, reads=[f"ps{4 + r}"], writes=[f"st16_{i}"])
                for hh in range(GC // 128):
                    h = g * (GC // 128) + hh
                    self.dma("sync", self.V_own[h * 128:(h + 1) * 128, jb * 128:(jb + 1) * 128],
                             st[:, hh * 128:(hh + 1) * 128], reads=[f"st16_{i}"], writes=[f"V{h}_{jb}"], key=f"st16_{i}_st")
                    for h2 in range(g * (GC // 128), (g + 1) * (GC // 128)):
                        self.kvkeys[h2].append(f"V{h}_{jb}")
        for jb in blocks:
            lc = jb * 128 - base
            c0t = self.tile_of(jb * 128, tiles)
            pf = self.ps[6]
            for k in range(KC):
                self.mm(pf[:, 0:H], self.xn[:, k * XW + lc:k * XW + lc + 128], self.wf_bf[:, k * H:(k + 1) * H],
                        k == 0, k == KC - 1, reads=["wf_bf", xk(k, c0t)], writes=["ps6"])
            b = self.rot("tmpb", 2)
            t = self.tmpb[b]
            S.add("vector", lambda e, t=t, pf=pf: e.tensor_tensor(out=t[:, 0:H], in0=pf[:, 0:H], in1=self.bfb[:, 0:H], op=ALU.add),
                  reads=["ps6", "b_f"], writes=[f"tmpb{b}"])
            self.act(t[:, 0:H], t[:, 0:H], AF.Exp, reads=[f"tmpb{b}"], writes=[f"tmpb{b}"], scale=-1.0)
            self.act(t[:, 0:H], t[:, 0:H], AF.Ln, reads=[f"tmpb{b}"], writes=[f"tmpb{b}"], bias=1.0, scale=1.0)
            dst = self.logf[:, jb * H:(jb + 1) * H]
            S.add("vector", lambda e, t=t, dst=dst: e.tensor_scalar(
                out=dst, in0=t[:, 0:H], scalar1=-1.0, scalar2=None, op0=ALU.mult),
                reads=[f"tmpb{b}"], writes=[f"logf{jb}"])

    def tile_of(self, col, tiles):
        for (c0, n) in tiles:
            if c0 <= col < c0 + n:
                return c0
        raise KeyError(col)

    def cumsum_phase(self):
        c = self.cfg
        H = c.H
        S = self.S
        for jb in range(c.NBLK):
            pc = self.ps[6]
            self.mm(pc[:, 0:H], self.tri32[:], self.logf[:, jb * H:(jb + 1) * H], True, jb == 0,
                    reads=["tri32", f"logf{jb}"], writes=["ps6"])
            for i in range(jb):
                self.mm(pc[:, 0:H], self.ones32[:], self.logf[:, i * H:(i + 1) * H], False, i == jb - 1,
                        reads=["ones32", f"logf{i}"], writes=["ps6"])
            S.add("vector", lambda e, pc=pc, jb=jb: e.tensor_copy(out=self.csb[:, jb * H:(jb + 1) * H], in_=pc[:, 0:H]),
                  reads=["ps6"], writes=["csb"])

    def _build_A(self):
        c = self.cfg
        KC, H = c.KC, c.H
        S = self.S
        NL = c.NL_A + c.NL_B
        o = 0
        self.off_an = o
        self.load_vec("a_norm", self.a_norm, c.NL_A * KC, o); o += c.NL_A * KC
        self.off_cv = o
        self.load_vec("a_conv", self.a_conv, c.NL_A * 3 * KC, o); o += c.NL_A * 3 * KC
        self.off_kvn = o
        self.load_vec("kv_norm", self.kv_norm, KC, o); o += KC
        self.off_kn = o
        self.load_vec("k_norm", self.k_norm, 1, o); o += 1
        self.off_ffn = o
        self.load_vec("ffn_norm", self.ffn_norm, NL * KC, o); o += NL * KC
        self.off_bn = o
        self.load_vec("b_norm", self.b_norm, c.NL_B * KC, o); o += c.NL_B * KC
        self.off_qn = o
        self.load_vec("b_q_norm", self.b_q_norm, c.NL_B, o); o += c.NL_B
        assert o <= 16 * KC + 64
        self.bfb = self.sb("bfb", [128, H], F32)
        self.dma("sync", self.bfb[:], self.b_f, [], ["b_f"], "b_f")
        self.tri32 = self.sb("tri32", [128, 128], F32)
        self.dma("sync", self.tri32[:], self.tri32_d, [], ["tri32"], "tri32")
        self.ones32 = self.sb("ones32", [128, 128], F32)
        S.add("vector", lambda e: e.memset(self.ones32[:], 1.0), writes=["ones32"])
        self.wf_bf = self.sb("wf_bf", [128, KC * H], BF16)
        self.dma("gpsimd", self.wf_bf[:], self.w_f, [], ["wf_bf"], "wf_bf")
        self.logf = self.sb("logf", [128, c.NBLK * H], F32)
        self.csb = self.sb("csb", [128, c.NBLK * H], F32)
        self.hkey_kv = lambda m, c0: [f"h{m}_112", f"hpad{m}"] if c0 == 0 else f"h{m}_{c0}"
        self.xnkey_kv = lambda k, c0: f"xn{k}_{c0}"
        for m in range(KC):
            i = self.rot("hs", 3)
            hs = self.hs[i]
            self.dma("sync", hs[:, 0:112], self.xT[m * 128:(m + 1) * 128, 0:112], [], [f"hs{i}"], f"hs{i}_ld")
            self.dma("sync", self.hT[m * 128:(m + 1) * 128, 0:112], hs[:, 0:112], [f"hs{i}"], [f"hpad{m}"], f"hs{i}_st")
        self.kvkeys = {h: [] for h in range(H)}
        for s in range(len(c.ST_A)):
            src, skey = self.xT, self.xkey
            for l in range(c.NL_A):
                self.mixer_a(l, s, src, skey)
                src, skey = self.hT, self.hkey
                self.ffn(l, s, src, skey)
            self.kv_phase(s)
        self.cumsum_phase()

    def _build_B(self):
        c = self.cfg
        KC, H = c.KC, c.H
        S = self.S
        NB = c.NBLK
        self.sel32 = self.sb("sel32", [128, 128], F32)
        self.dma("sync", self.sel32[:], self.sel32_d, [], ["sel32"], "sel32")
        self.trim = self.sb("trim", [128, 128], BF16)
        self.dma("gpsimd", self.trim[:], self.trim_d, [], ["trim"], "trim")
        self.koff_sb = self.sb("koff_sb", [128, NB], F32)
        self.dma("sync", self.koff_sb[:], self.koff, [], ["koff"], "koff")
        self.cneg = self.sb("cneg", [128, H * NB], F32)
        NTH = c.NTILE * 2
        self.cref = self.sb("cref", [128, NTH * H], F32)
        self.biasb = [self.sb(f"biasb{i}", [128, 2 * NB], F32) for i in range(4)]
        self.rD = self.sb("rD", [128, 512], F32)
        self.pbuf = [self.sb(f"pbuf{i}", [128, c.TT], BF16) for i in range(4)]
        pz = self.ps[6]
        co3 = self.csb[:].rearrange("p (j h) -> p j h", h=H)
        for h in range(H):
            dd = self.cneg[:, h * NB:(h + 1) * NB]
            S.add("vector", lambda e, dd=dd, h=h: e.tensor_scalar(
                out=dd, in0=co3[:, :, h], scalar1=-1.0, scalar2=None, op0=ALU.mult),
                reads=["csb"], writes=["cneg"])
            S.add("vector", lambda e, dd=dd: e.tensor_tensor(out=dd, in0=dd, in1=self.koff_sb[:], op=ALU.subtract),
                  reads=["cneg", "koff"], writes=["cneg"])
        for t in range(c.NTILE):
            for hf in range(2):
                jb = c.PB + (c.TT // 128) * t + (c.TT // 256) * hf
                th = t * 2 + hf
                self.mm(pz[:, 0:H], self.sel32[:], self.csb[:, jb * H:(jb + 1) * H], True, True, ["sel32", "csb"], ["ps6"])
                S.add("vector", lambda e, th=th: e.tensor_copy(out=self.cref[:, th * H:(th + 1) * H], in_=pz[:, 0:H]),
                      reads=["ps6"], writes=["cref"])
        src, skey = self.hT, self.hkey
        for l in range(c.NL_B):
            for s in range(len(c.ST_B)):
                self.mixer_b(l, s, src, skey)
                last = l == c.NL_B - 1
                self.ffn(c.NL_A + l, s, self.hT, self.hkey, dst_final=self.outT if last else None)
        allw = [f"out{m}_{c0}" for m in range(KC) for s in range(len(c.ST_B)) for (c0, n) in c.tiles(s)]
        S.add("sync", None, reads=allw)

    def mixer_b(self, l, s, src, srckeyf):
        c = self.cfg
        KC, H, NB, TT = c.KC, c.H, c.NBLK, c.TT
        S = self.S
        XW = c.XW
        base, _ = c.ST[s]
        tiles = c.tiles(s)
        self.norm(src, srckeyf, "b_norm", self.off_bn + l * KC, tiles, base)
        XB = c.XWB
        OQ = H * XB
        OKV = (H + 2) * XB
        KVW = NB * 128
        scale = 1.0 / float(np.sqrt(128.0))
        SUB = TT // 128
        def emit_bias(hh):
            for ti, (c0, n) in enumerate(tiles):
                t = (c0 - c.OWN0) // TT
                bi_ = (hh % 2) * 2 + (ti % 2)
                bb = self.biasb[bi_]
                for hf in range(2):
                    th = t * 2 + hf
                    S.add("vector", lambda e, bb=bb, th=th, hh=hh, hf=hf: e.tensor_scalar(
                        out=bb[:, hf * NB:(hf + 1) * NB], in0=self.cneg[:, hh * NB:(hh + 1) * NB],
                        scalar1=self.cref[:, th * H + hh:th * H + hh + 1], scalar2=None, op0=ALU.add),
                        reads=["cneg", "cref"], writes=[f"biasb{bi_}"])
        emit_bias(0)
        for h in range(H):
            if h + 1 < H:
                emit_bias(h + 1)
            wt, wkey = self.load_w(self.b_w_q[l * H + h], KC * 128, "q")
            qs = self.rot("qslot", 2)
            kvs = self.rot("kvslot", 2)
            kt = self.big[:, OKV + kvs * 2 * KVW:OKV + kvs * 2 * KVW + KVW]
            vt = self.big[:, OKV + kvs * 2 * KVW + KVW:OKV + (kvs + 1) * 2 * KVW]
            kkey, vkey = f"kt{kvs}", f"vt{kvs}"
            self.dma("sync", kt, self.KT_own[h * 128:(h + 1) * 128, :], self.kvkeys[h], [kkey], kkey)
            self.dma("sync", vt, self.V_own[h * 128:(h + 1) * 128, :], self.kvkeys[h], [vkey], vkey)
            pending = None
            for (c0, n) in tiles:
                lc = c0 - base
                r = self.rot("qps", 2)
                pq = self.ps[r]
                for k in range(KC):
                    self.mm(pq[:, 0:n], wt[:, k * 128:(k + 1) * 128], self.xn[:, k * XW + lc:k * XW + lc + n],
                            k == 0, k == KC - 1, reads=[wkey, f"xn{k}_{c0}"], writes=[f"ps{r}"])
                if pending is not None:
                    pending()

                def fin(pq=pq, r=r, n=n, lc=lc, c0=c0, qs=qs):
                    self.qk_norm_store(pq, f"ps{r}", n, self.gvec[:, self.off_qn + l:self.off_qn + l + 1], "b_q_norm",
                                       self.big[:, OQ + qs * XB + lc:OQ + qs * XB + lc + n], f"q{qs}_{c0}")
                pending = fin
            if pending is not None:
                pending()
            for ti, (c0, n) in enumerate(tiles):
                lc = c0 - base
                t = (c0 - c.OWN0) // TT
                qap = self.big[:, OQ + qs * XB + lc:OQ + qs * XB + lc + n]
                qkey = f"q{qs}_{c0}"
                bi_ = (h % 2) * 2 + (ti % 2)
                bb = self.biasb[bi_]
                bbkey = f"biasb{bi_}"
                ND = c.PB + SUB * t
                blks = [(j, 0) for j in range(ND)] + [(ND + i, i) for i in range(SUB)]
                ar = self.rot("attps", 2)
                pO, pD = self.ps[2 + ar], self.ps[4 + ar]
                okey, dkey = f"ps{2 + ar}", f"ps{4 + ar}"
                nb = len(blks)

                def issue_S(bi):
                    gb, i = blks[bi]
                    qa = 128 * i
                    sr = (6, 7, 1, 0)[self.rot("sps", 4)]
                    pS = self.ps[sr]
                    skey_ = f"ps{sr}"
                    self.mm(pS[:, qa:n], kt[:, gb * 128:(gb + 1) * 128], qap[:, qa:n], True, True,
                            reads=[kkey, qkey], writes=[skey_])
                    return pS, skey_

                def issue_rest(bi, pS, skey_):
                    gb, i = blks[bi]
                    qa = 128 * i
                    pi = self.rot("pbuf", 4)
                    P = self.pbuf[pi]
                    pkey = f"pbuf{pi}"
                    for hf in range(2):
                        lo = max(qa, 256 * hf)
                        hi = 256 * (hf + 1)
                        if lo >= hi:
                            continue
                        self.act(P[:, lo:hi], pS[:, lo:hi], AF.Exp, reads=[skey_, bbkey], writes=[pkey],
                                 bias=bb[:, hf * NB + gb:hf * NB + gb + 1], scale=scale)
                    if gb >= ND:
                        S.add("gpsimd", lambda e, P=P, qa=qa: e.tensor_tensor(
                            out=P[:, qa:qa + 128], in0=P[:, qa:qa + 128], in1=self.trim[:], op=ALU.mult),
                            reads=[pkey, "trim"], writes=[pkey])
                    self.mm(pO[:, qa:n], vt[:, gb * 128:(gb + 1) * 128], P[:, qa:n], bi == 0, bi == nb - 1,
                            reads=[vkey, pkey], writes=[okey])
                    self.mm(pD[:, qa:n], self.ones_bf[:], P[:, qa:n], bi == 0, bi == nb - 1,
                            reads=["ones_bf", pkey], writes=[dkey])

                LA = 3
                sq_ = [issue_S(i) for i in range(min(LA, nb))]
                for bi in range(nb):
                    if bi + LA < nb:
                        sq_.append(issue_S(bi + LA))
                    issue_rest(bi, *sq_[bi])
                S.add("vector", lambda e, pD=pD, n=n: e.reciprocal(out=self.rD[:, 0:n], in_=pD[:, 0:n]),
                      reads=[dkey], writes=["rD"])
                dst = self.big[:, h * XB + lc:h * XB + lc + n]
                S.add("vector", lambda e, dst=dst, pO=pO, n=n: e.tensor_tensor(
                    out=dst, in0=pO[:, 0:n], in1=self.rD[:, 0:n], op=ALU.mult),
                    reads=[okey, "rD"], writes=[f"big{h}_{c0}"])
        for m in range(KC):
            wt, wkey = self.load_w(self.b_w_o[l * KC + m], KC * 128, "o")
            for (c0, n) in tiles:
                lc = c0 - base
                r = self.rot("kps", 2)
                po = self.ps[r]
                for k in range(KC):
                    self.mm(po[:, 0:n], wt[:, k * 128:(k + 1) * 128], self.big[:, k * XB + lc:k * XB + lc + n],
                            k == 0, k == KC - 1, reads=[wkey, f"big{k}_{c0}"], writes=[f"ps{r}"])
                self.resid(po, f"ps{r}", src, srckeyf, self.hT, self.hkey, m, c0, n)


def _consts():
    i = np.arange(128)
    tri = (i[:, None] <= i[None, :]).astype(np.float32)
    sel = np.zeros((128, 128), np.float32)
    sel[127, :] = 1.0
    return tri, sel


def run_fused(cfg, x, meta, a_norm, a_w_in, a_conv, a_w_out, kv_norm, w_kv, k_norm, w_f, b_f,
              b_norm, b_w_q, b_q_norm, b_w_o, ffn_norm, ffn_w_gu, ffn_w_down, n_cores=8):
    c = cfg
    D, F, KC, FC, H = c.D, c.F, c.KC, c.FC, c.H
    B = n_cores // 2
    NM = c.NMAIN
    NB = c.NBLK
    f32 = np.float32
    x = np.asarray(x, f32)
    meta = np.asarray(meta, f32)
    tri, sel = _consts()
    LA, LB = c.NL_A, c.NL_B
    shared = {
        "a_norm_T": vecT(a_norm),
        "a_w_in_b": np.concatenate([blk(np.asarray(a_w_in[l], f32)) for l in range(LA)], 0),
        "a_conv_T": vecT(np.asarray(a_conv, f32).reshape(LA * 3, D)),
        "a_w_out_b": np.concatenate([blk(np.asarray(a_w_out[l], f32)) for l in range(LA)], 0),
        "kv_norm_T": vecT(kv_norm),
        "w_k_b": blk(np.asarray(w_kv, f32)[:, :D]),
        "w_v_b": blk(np.asarray(w_kv, f32)[:, D:], c.VGC),
        "k_norm_T": np.ascontiguousarray(np.asarray(k_norm, f32).reshape(128, 1)),
        "w_f_l": np.ascontiguousarray(np.asarray(w_f, f32).reshape(KC, 128, H).transpose(1, 0, 2).reshape(128, KC * H)),
        "b_f_b": np.ascontiguousarray(np.broadcast_to(np.asarray(b_f, f32)[None, :], (128, H))),
        "tri32": tri,
        "b_norm_T": vecT(b_norm),
        "b_w_q_b": np.concatenate([blk(np.asarray(b_w_q[l], f32)) for l in range(LB)], 0),
        "b_q_norm_T": np.ascontiguousarray(np.asarray(b_q_norm, f32).T),
        "b_w_o_b": np.concatenate([blk(np.asarray(b_w_o[l], f32)) for l in range(LB)], 0),
        "sel32": sel,
        "trimask": tri,
        "ffn_norm_T": vecT(np.asarray(ffn_norm, f32)),
        "ffn_w_gu_b": np.concatenate([blk(np.asarray(ffn_w_gu[l], f32)) for l in range(LA + LB)], 0),
        "ffn_w_down_b": np.concatenate([blk(np.asarray(ffn_w_down[l], f32)) for l in range(LA + LB)], 0),
    }
    in_maps = []
    for core in range(n_cores):
        b, half = core // 2, core % 2
        xT = np.zeros((D, c.NCOL), f32)
        koff = np.zeros((128, NB), f32)
        if half == 0:
            xT[:, c.OWN0 - 16:c.OWN0] = meta.T
            xT[:, c.OWN0:] = x[b, 0:NM].T
            koff[:, :c.PB - 1] = MASKV
            koff[:112, c.PB - 1] = MASKV
        else:
            xT[:, 112:128] = meta.T
            xT[:, 128:] = x[b].T
            koff[:112, 0] = MASKV
        m = dict(shared)
        m["xT"] = xT
        m["koff"] = koff
        in_maps.append(m)
    nc = Prog(c).build()
    res = run_bass_kernel_spmd(nc, in_maps, core_ids=list(range(n_cores))).results
    out = np.empty((B, 2 * NM, D), f32)
    for core in range(n_cores):
        b, half = core // 2, core % 2
        out[b, half * NM:(half + 1) * NM] = np.asarray(res[core]["outT"]).T
    return out


def kernel(**inputs):
    cfg = Cfg()
    return run_fused(cfg, **inputs)
```
